# Optimizing a Trainium2 kernel written in Bass

```python
import jax, jax.numpy as jnp
from jax import lax
import numpy as np


D_MODEL = 1024
BATCH = 8
SEQ = 2048
DEPTH = 1
DEC_BATCH = 128
DEC_SEQ = 4
PAST_LEN = 16384
PAGE_SIZE = 128

N_RET_HEADS = 4
RET_HEAD_DIM = 256
D_RET = N_RET_HEADS * RET_HEAD_DIM
CHUNK = 128
D_CONV = D_MODEL
CONV_WIDTH = 31
N_MEM = 256
N_XA_HEADS = 4
XA_HEAD_DIM = 256
D_XA = N_XA_HEADS * XA_HEAD_DIM
N_BRANCH = 3
D_FF = 4 * D_MODEL
ROPE_BASE = 10000.0
EPS = 1e-6
D_IN = 4 * D_RET + 2 * D_CONV + D_XA + N_BRANCH * D_MODEL
SPLITS = (D_RET, 2 * D_RET, 3 * D_RET, 4 * D_RET, 4 * D_RET + 2 * D_CONV, 4 * D_RET + 2 * D_CONV + D_XA)

kernel_name = "gated_retention_conformer_memory_decoder_step"


def _rmsnorm(x, g):
    xf = x.astype(jnp.float32)
    y = xf * lax.rsqrt(jnp.mean(xf * xf, axis=-1, keepdims=True) + EPS) * g.astype(jnp.float32)
    return y.astype(x.dtype)


def _standardize(x):
    xf = x.astype(jnp.float32)
    mu = jnp.mean(xf, axis=-1, keepdims=True)
    xc = xf - mu
    return xc * lax.rsqrt(jnp.mean(xc * xc, axis=-1, keepdims=True) + EPS)


def _rope(x, pos):
    d = x.shape[-1]
    inv = ROPE_BASE ** (-jnp.arange(0, d, 2, dtype=jnp.float32) / d)
    ang = pos.astype(jnp.float32)[:, None] * inv[None, :]
    cos = jnp.cos(ang)[None, :, None, :]
    sin = jnp.sin(ang)[None, :, None, :]
    xf = x.astype(jnp.float32)
    x1, x2 = xf[..., : d // 2], xf[..., d // 2:]
    return jnp.concatenate([x1 * cos - x2 * sin, x2 * cos + x1 * sin], axis=-1)


def _log_gammas():
    return jnp.log1p(-jnp.exp2(-5.0 - jnp.arange(N_RET_HEADS, dtype=jnp.float32)))


def _retention(q, k, v, s0, log_gamma):
    B, H, L, _ = q.shape
    c = CHUNK if L % CHUNK == 0 else L
    n = L // c
    idx = jnp.arange(c, dtype=jnp.float32)
    diff = idx[:, None] - idx[None, :]
    causal = diff >= 0
    intra_decay = jnp.where(causal[None], jnp.exp(log_gamma[:, None, None] * jnp.where(causal, diff, 0.0)[None]), 0.0)
    q_decay = jnp.exp(log_gamma[:, None] * (idx + 1.0))[None, :, :, None]
    k_decay = jnp.exp(log_gamma[:, None] * (c - 1.0 - idx))[None, :, :, None]
    chunk_decay = jnp.exp(log_gamma * c)[None, :, None, None]

    def step(s, qkv):
        qc, kc, vc = qkv
        scores = jnp.einsum('bhid,bhjd->bhij', qc, kc) * intra_decay[None]
        o = jnp.einsum('bhij,bhjv->bhiv', scores, vc) + jnp.einsum('bhid,bhdv->bhiv', qc, s) * q_decay
        s = s * chunk_decay + jnp.einsum('bhjd,bhjv->bhdv', kc * k_decay, vc)
        return s, o

    def to_chunks(t):
        return jnp.moveaxis(t.reshape(B, H, n, c, t.shape[-1]), 2, 0)

    s_final, o = lax.scan(step, s0, (to_chunks(q), to_chunks(k), to_chunks(v)))
    o = jnp.moveaxis(o, 0, 2).reshape(B, H, L, v.shape[-1])
    return o, s_final


def _conformer_conv(glu_in, conv_buf, conv_w, conv_b, ln_g, ln_b, w_conv_out):
    a, gt = jnp.split(glu_in, 2, axis=-1)
    u = a * jax.nn.sigmoid(gt)
    ext = jnp.concatenate([conv_buf.astype(u.dtype), u], axis=1)
    y = lax.conv_general_dilated(ext, conv_w[:, None, :].astype(u.dtype), (1,), 'VALID',
                                 dimension_numbers=('NWC', 'WIO', 'NWC'),
                                 feature_group_count=D_CONV) + conv_b
    y = _standardize(y) * ln_g.astype(jnp.float32) + ln_b.astype(jnp.float32)
    y = jax.nn.silu(y).astype(glu_in.dtype) @ w_conv_out
    return y, ext[:, -(CONV_WIDTH - 1):]


def _cross_attention(q, mem_k, mem_v):
    s = jnp.einsum('blhd,bnhd->bhln', q.astype(jnp.float32), mem_k.astype(jnp.float32)) * (XA_HEAD_DIM ** -0.5)
    p = jax.nn.softmax(s, axis=-1)
    o = jnp.einsum('bhln,bnhd->blhd', p, mem_v.astype(jnp.float32))
    return o.astype(q.dtype)


def _memory_kv(mem, g_mem, w_mem_kv):
    B, N, _ = mem.shape
    kv = (_rmsnorm(mem, g_mem) @ w_mem_kv).reshape(B, N, 2, N_XA_HEADS, XA_HEAD_DIM)
    return kv[:, :, 0], kv[:, :, 1]


def _layer(x, pos, s_ret, conv_buf, mem_k, mem_v, g_mix, w_in, ret_gn_g, conv_w, conv_b,
           conv_ln_g, conv_ln_b, w_conv_out, w_out, g_ffn, w_up, w_down):
    B, L, _ = x.shape
    h = _rmsnorm(x, g_mix)
    proj = h @ w_in
    q, k, v, g, glu_in, q_xa, gate_logits = jnp.split(proj, SPLITS, axis=-1)

    qr = _rope(q.reshape(B, L, N_RET_HEADS, RET_HEAD_DIM), pos)
    kr = _rope(k.reshape(B, L, N_RET_HEADS, RET_HEAD_DIM), pos) * (RET_HEAD_DIM ** -0.5)
    vr = v.reshape(B, L, N_RET_HEADS, RET_HEAD_DIM).astype(jnp.float32)
    o, s_new = _retention(qr.transpose(0, 2, 1, 3), kr.transpose(0, 2, 1, 3), vr.transpose(0, 2, 1, 3),
                          s_ret.astype(jnp.float32), _log_gammas())
    o = _standardize(o.transpose(0, 2, 1, 3)).reshape(B, L, D_RET) * ret_gn_g.astype(jnp.float32)
    ret_out = (o * jax.nn.silu(g.astype(jnp.float32))).astype(x.dtype)

    conv_out, buf_new = _conformer_conv(glu_in, conv_buf, conv_w, conv_b, conv_ln_g, conv_ln_b, w_conv_out)

    xa_out = _cross_attention(q_xa.reshape(B, L, N_XA_HEADS, XA_HEAD_DIM), mem_k, mem_v).reshape(B, L, D_XA)

    gates = jax.nn.sigmoid(gate_logits).reshape(B, L, N_BRANCH, D_MODEL)
    merged = gates[:, :, 0] * ret_out + gates[:, :, 1] * conv_out + gates[:, :, 2] * xa_out
    x = x + merged @ w_out

    h2 = _rmsnorm(x, g_ffn)
    x = x + jnp.square(jax.nn.relu(h2 @ w_up)) @ w_down
    return x, s_new.astype(x.dtype), buf_new


def setup_inputs(seed: int = 0) -> dict:
    key = jax.random.key(seed)
    ks = jax.random.split(key, 24)

    def nrm(k, shape, scale):
        return jax.random.normal(k, shape, jnp.float32) * scale

    def gain(k, shape):
        return 1.0 + 0.02 * jax.random.normal(k, shape, jnp.float32)

    return {
        "x_prompt": nrm(ks[0], (BATCH, SEQ, D_MODEL), 1.0),
        "x_sample": nrm(ks[1], (DEC_BATCH, DEC_SEQ, D_MODEL), 1.0),
        "mem_prompt": nrm(ks[2], (BATCH, N_MEM, D_MODEL), 1.0),
        "state_ret": nrm(ks[3], (DEPTH, DEC_BATCH, N_RET_HEADS, RET_HEAD_DIM, RET_HEAD_DIM), 0.5),
        "state_conv": nrm(ks[4], (DEPTH, DEC_BATCH, CONV_WIDTH - 1, D_CONV), 0.5),
        "cache_mem_k": nrm(ks[5], (DEPTH, DEC_BATCH, N_MEM, N_XA_HEADS, XA_HEAD_DIM), 1.0),
        "cache_mem_v": nrm(ks[6], (DEPTH, DEC_BATCH, N_MEM, N_XA_HEADS, XA_HEAD_DIM), 1.0),
        "g_mix": gain(ks[7], (DEPTH, D_MODEL)),
        "w_in": nrm(ks[8], (DEPTH, D_MODEL, D_IN), D_MODEL ** -0.5),
        "ret_gn_g": gain(ks[9], (DEPTH, D_RET)),
        "conv_w": nrm(ks[10], (DEPTH, CONV_WIDTH, D_CONV), CONV_WIDTH ** -0.5),
        "conv_b": nrm(ks[11], (DEPTH, D_CONV), 0.02),
        "conv_ln_g": gain(ks[12], (DEPTH, D_CONV)),
        "conv_ln_b": nrm(ks[13], (DEPTH, D_CONV), 0.02),
        "w_conv_out": nrm(ks[14], (DEPTH, D_CONV, D_MODEL), D_CONV ** -0.5),
        "w_out": nrm(ks[15], (DEPTH, D_MODEL, D_MODEL), D_MODEL ** -0.5),
        "g_ffn": gain(ks[16], (DEPTH, D_MODEL)),
        "w_up": nrm(ks[17], (DEPTH, D_MODEL, D_FF), D_MODEL ** -0.5),
        "w_down": nrm(ks[18], (DEPTH, D_FF, D_MODEL), D_FF ** -0.5),
        "g_mem": gain(ks[19], (DEPTH, D_MODEL)),
        "w_mem_kv": nrm(ks[20], (DEPTH, D_MODEL, 2 * D_XA), D_MODEL ** -0.5),
        "g_final": gain(ks[21], (D_MODEL,)),
    }


def reference(x_prompt, x_sample, mem_prompt, state_ret, state_conv, cache_mem_k, cache_mem_v,
              g_mix, w_in, ret_gn_g, conv_w, conv_b, conv_ln_g, conv_ln_b, w_conv_out, w_out,
              g_ffn, w_up, w_down, g_mem, w_mem_kv, g_final):
    b_p, l_p = x_prompt.shape[0], x_prompt.shape[1]
    l_s = x_sample.shape[1]
    pos_p = jnp.arange(l_p)
    pos_s = PAST_LEN + jnp.arange(l_s)
    hp, hs = x_prompt, x_sample
    ret_p, conv_p, mk_p, mv_p, ret_s, conv_s = [], [], [], [], [], []
    for l in range(DEPTH):
        w = (g_mix[l], w_in[l], ret_gn_g[l], conv_w[l], conv_b[l], conv_ln_g[l], conv_ln_b[l],
             w_conv_out[l], w_out[l], g_ffn[l], w_up[l], w_down[l])
        mk, mv = _memory_kv(mem_prompt, g_mem[l], w_mem_kv[l])
        s0 = jnp.zeros((b_p, N_RET_HEADS, RET_HEAD_DIM, RET_HEAD_DIM), jnp.float32)
        buf0 = jnp.zeros((b_p, CONV_WIDTH - 1, D_CONV), x_prompt.dtype)
        hp, sp, cp = _layer(hp, pos_p, s0, buf0, mk, mv, *w)
        hs, ss, cs = _layer(hs, pos_s, state_ret[l], state_conv[l], cache_mem_k[l], cache_mem_v[l], *w)
        ret_p.append(sp)
        conv_p.append(cp)
        mk_p.append(mk)
        mv_p.append(mv)
        ret_s.append(ss)
        conv_s.append(cs)
    y_prompt = _rmsnorm(hp, g_final)
    y_sample = _rmsnorm(hs, g_final)
    return (y_prompt, y_sample, jnp.stack(ret_p), jnp.stack(conv_p), jnp.stack(mk_p), jnp.stack(mv_p),
            jnp.stack(ret_s), jnp.stack(conv_s))
```

```python
import numpy as np
from contextlib import ExitStack
import concourse.bass as bass
import concourse.mybir as mybir
from concourse.bass_utils import run_bass_kernel_spmd

dt = mybir.dt
F32 = dt.float32
BF16 = dt.bfloat16
AF = mybir.ActivationFunctionType
ALU = mybir.AluOpType

NCORES = 8
D = 1024
DIN = 10240
DFF = 4096
T = 2048
NT = 512
NTILES = T // NT
NH = 4
HD = 256
NMEM = 256
CW = 31
PAST = 16384
SB = 16
DS = 4
NS = SB * DS
EPS = 1e-6
NCV = 96

DEBUG_OUT = {}
STAGE = ['init']
LAST_SCHED = [None]
DEBUG = False


class Op:
    __slots__ = ("eng", "fn", "r", "w", "dma", "eidx", "signal", "waits", "sem", "val", "ringwait", "marker", "barrier", "stage")


class Sched:
    ENGS = ("pe", "act", "dve", "pool", "sp")

    def __init__(self):
        self.ops = []

    def _mk(self, eng, fn, r, w, dma):
        o = Op()
        o.eng, o.fn, o.r, o.w, o.dma = eng, fn, tuple(r), tuple(w), dma
        o.signal = False
        o.waits = []
        o.marker = False
        o.barrier = False
        o.stage = STAGE[0]
        o.ringwait = None
        return o

    def op(self, eng, fn, r=(), w=()):
        o = self._mk(eng, fn, r, w, False)
        self.ops.append(o)
        return o

    def dma(self, q, fn, r=(), w=()):
        o = self._mk(q, fn, r, w, True)
        self.ops.append(o)
        return o

    def make_dma(self, q, fn, r=(), w=()):
        return self._mk(q, fn, r, w, True)

    def marker(self):
        o = self._mk(None, None, (), (), False)
        o.marker = True
        self.ops.append(o)
        return o

    def barrier(self):
        for e in ("pe", "act", "dve", "pool", "sp"):
            o = self._mk(e, (lambda en: en.nop()), (), (), False)
            o.barrier = True
            self.ops.append(o)

    def insert_after(self, marker, op):
        i = self.ops.index(marker)
        self.ops.insert(i + 1, op)

    def analyze(self):
        self.ops = [o for o in self.ops if not o.marker]
        import os
        lim = int(os.environ.get('K_LIMIT', 10**9))
        self.ops = self.ops[:lim]
        eng_ops = {e: [] for e in self.ENGS}
        for o in self.ops:
            o.eidx = len(eng_ops[o.eng])
            eng_ops[o.eng].append(o)
        self.eng_ops = eng_ops
        last_w = {}
        readers = {}
        waited = {e: {} for e in self.ENGS}
        for o in self.ops:
            deps = []
            for t in o.r:
                d = last_w.get(t)
                if d is not None:
                    deps.append(d)
                if isinstance(t, tuple) and t[0] == "ps":
                    deps.extend(x for x in readers.get(t, ()) if x.eng != o.eng)
            for t in o.w:
                d = last_w.get(t)
                if d is not None:
                    deps.append(d)
                deps.extend(readers.get(t, ()))
            if o.barrier:
                pos = self.ops.index(o)
                seen_dma = {}
                for x in self.ops[:pos]:
                    if x.barrier:
                        continue
                    if x.dma:
                        seen_dma.setdefault(x.eng, []).append(x)
                    elif x.eng != o.eng:
                        deps.append(x) if False else None
                lastc = {}
                for x in self.ops[:pos]:
                    if not x.dma and not x.barrier and x.eng != o.eng:
                        lastc[x.eng] = x
                deps.extend(lastc.values())
                for q, lst in seen_dma.items():
                    deps.extend(lst[-8:])
            need = {}
            for d in deps:
                if d is o:
                    continue
                if d.dma:
                    need[("dma", id(d))] = d
                else:
                    if d.eng == o.eng and not o.dma:
                        if o.eng == "pe":
                            continue
                        if o.eng != "pool" and o.eidx - d.eidx > 3:
                            continue
                    k = d.eng
                    if k not in need or need[k].eidx < d.eidx:
                        need[k] = d
            wd = waited[o.eng]
            for k, d in need.items():
                if d.dma:
                    if k in wd:
                        continue
                    wd[k] = True
                else:
                    if wd.get(k, -1) >= d.eidx:
                        continue
                    wd[k] = d.eidx
                d.signal = True
                o.waits.append(d)
            for t in o.r:
                readers.setdefault(t, []).append(o)
            for t in o.w:
                last_w[t] = o
                readers[t] = []

    def emit(self, nc, es):
        RING = 8
        sems = {e: es.enter_context(nc.semaphore("sem_" + e)) for e in ("pe", "act", "dve", "pool")}
        rings = {q: [es.enter_context(nc.semaphore("ring_%s_%d" % (q, i))) for i in range(RING)]
                 for q in ("sp", "pool", "act")}
        for e in ("pe", "act", "dve", "pool"):
            cnt = 0
            for o in self.eng_ops[e]:
                if o.dma:
                    continue
                if o.signal:
                    cnt += 1
                    o.sem, o.val = sems[e], cnt
        all_dma = []
        for q in ("sp", "pool", "act"):
            n = 0
            for o in self.eng_ops[q]:
                if not o.dma:
                    continue
                o.sem = rings[q][n % RING]
                o.val = 16 * (n // RING + 1)
                if n >= RING:
                    o.ringwait = (o.sem, 16 * (n // RING))
                n += 1
                all_dma.append(o)
        final = {}
        for o in all_dma:
            final[id(o.sem)] = (o.sem, max(final.get(id(o.sem), (None, 0))[1], o.val))

        def run(name, e):
            for o in self.eng_ops[name]:
                if o.ringwait is not None:
                    e.wait_ge(o.ringwait[0], o.ringwait[1])
                for d in o.waits:
                    e.wait_ge(d.sem, d.val)
                ins = o.fn(e)
                if o.dma:
                    ins.then_inc(o.sem, 16)
                elif o.signal:
                    ins.then_inc(o.sem, 1)
            if name == "sp":
                for sem, val in final.values():
                    e.wait_ge(sem, val)

        with nc.Block() as block:
            @block.tensor
            def _(e):
                run("pe", e)

            @block.scalar
            def _(e):
                run("act", e)

            @block.vector
            def _(e):
                run("dve", e)

            @block.gpsimd
            def _(e):
                run("pool", e)

            @block.sync
            def _(e):
                run("sp", e)


class Arena:
    def __init__(self, ap, nwords):
        self.ap = ap
        self.n = nwords
        self.off = 0
        self.hi = 0

    def tile(self, free_shape, dtype, parts=128):
        nel = int(np.prod(free_shape))
        nw = nel if dtype == F32 else (nel + 1) // 2
        nw = (nw + 7) // 8 * 8
        assert self.off + nw <= self.n, ("arena overflow", self.off, nw, self.n)
        v = self.ap[:, self.off:self.off + nw]
        self.off += nw
        self.hi = max(self.hi, self.off)
        if dtype != F32:
            v = v.bitcast(dtype)
        v = v[:, 0:nel]
        if len(free_shape) == 2:
            v = v.rearrange("p (a b) -> p a b", a=free_shape[0])
        elif len(free_shape) == 3:
            v = v.rearrange("p (a b c) -> p a b c", a=free_shape[0], b=free_shape[1])
        if parts != 128:
            v = v[0:parts]
        return v


def log_gammas():
    return np.log1p(-np.exp2(-5.0 - np.arange(NH, dtype=np.float64)))


def build(ntiles=None):
    import os
    if ntiles is None:
        ntiles = int(os.environ.get('K_NTILES', NTILES))
    nc = bass.Bass("TRN2", target_bir_lowering=False)
    es = ExitStack()

    def din(name, shape):
        return nc.dram_tensor(name, list(shape), F32, kind="ExternalInput").ap()

    def dout(name, shape):
        return nc.dram_tensor(name, list(shape), F32, kind="ExternalOutput").ap()

    xp = din("xp", [T, D])
    xsm = din("xsm", [NS, D])
    mem = din("mem", [NMEM, D])
    sret = din("sret", [SB, NH, HD, HD])
    sconv = din("sconv", [SB, 30, D])
    ck = din("ck", [SB, NMEM, D])
    cv = din("cv", [SB, NMEM, D])
    w_in = din("w_in", [D, DIN])
    w_co = din("w_co", [D, D])
    w_out = din("w_out", [D, D])
    w_up = din("w_up", [D, DFF])
    w_dn = din("w_dn", [DFF, D])
    w_kv = din("w_kv", [D, 2 * D])
    dgw = din("dgw", [8, 128, CW, 128])
    cvec = din("cvec", [128, NCV])
    rowb = din("rowb", [128, 2 * D])
    cosp = din("cosp", [128, T])
    sinp = din("sinp", [128, T])
    css = din("css", [128, 2, NS])
    maskp = din("maskp", [128, NH, 128])
    masks = din("masks", [128, NH, 64])
    identf = din("identf", [128, 128])
    selm = din("selm", [128, SB, 64])

    yp = dout("yp", [T, D])
    ysm = dout("ysm", [NS, D])
    srp = dout("srp", [NH, HD, HD])
    scp = dout("scp", [30, D])
    mkp = dout("mkp", [NMEM, D])
    mvp = dout("mvp", [NMEM, D])
    srs = dout("srs", [SB, NH, HD, HD])
    scs = dout("scs", [SB, 30, D])

    S = Sched()
    dbg_n = [0]

    def dbg_dump(name, ap4, r):
        if not DEBUG:
            return
        d = dout("dbg_" + name, [NT, D])
        S.dma("sp", lambda e: e.dma_start(out=d.rearrange("(c p) f -> p c f", p=128), in_=ap4), r=r, w=[("dbg", name)])
    LG = log_gammas()
    GC = [float(np.exp(LG[h] * 128)) for h in range(NH)]
    G4 = [float(np.exp(LG[h] * DS)) for h in range(NH)]

    NW = 53000
    arena_t = es.enter_context(nc.sbuf_tensor("arena", [128, NW], F32))
    A = Arena(arena_t[:], NW)
    PS = [es.enter_context(nc.psum_tensor("ps%d" % i, [128, 512], F32))[:] for i in range(8)]
    PSB = [p.bitcast(BF16) for p in PS]

    ident_f = A.tile([128], F32)
    ident_b = A.tile([128], BF16)
    ones_f = A.tile([128], F32)
    cvt = A.tile([NCV], F32)
    rowbt = A.tile([2 * D], F32)
    maskt = A.tile([NH, 128], F32)
    maskst = A.tile([NH, 64], F32)
    cst_s = A.tile([2, NS], F32)
    selt = A.tile([SB, 64], BF16)
    C_GMIX, C_GFFN, C_GMEM, C_CB, C_LNG, C_LNB = 0, 8, 16, 24, 32, 40
    C_QD, C_QD2, C_KDEC = 48, 52, 56
    C_SQD, C_SQD2, C_SKDEC = 60, 64, 68
    C_SEL = 72

    S.dma("sp", lambda e: e.dma_start(out=ident_f, in_=identf), w=["c_identf"])
    S.dma("pool", lambda e: e.dma_start(out=ident_b, in_=identf), w=["c_identb"])
    S.dma("pool", lambda e: e.dma_start(out=selt, in_=selm), w=["c_sel"])
    S.dma("sp", lambda e: e.dma_start(out=cvt, in_=cvec), w=["c_cv"])
    S.dma("sp", lambda e: e.dma_start(out=rowbt, in_=rowb), w=["c_rowb"])
    S.dma("sp", lambda e: e.dma_start(out=maskt, in_=maskp), w=["c_mask"])
    S.dma("sp", lambda e: e.dma_start(out=maskst, in_=masks), w=["c_masks"])
    S.dma("sp", lambda e: e.dma_start(out=cst_s, in_=css), w=["c_css"])
    S.op("dve", lambda e: e.memset(ones_f, 1.0), w=["c_ones"])
    CONST = ["c_identf", "c_identb", "c_cv", "c_rowb", "c_mask", "c_masks", "c_css", "c_ones", "c_sel"]

    NSLOT = 4
    wbuf = [A.tile([8, 512], BF16) for _ in range(NSLOT)]
    dgbuf = [A.tile([CW, 128], BF16) for _ in range(2)]
    NTMP = 6
    tmp = [A.tile([512], F32) for _ in range(NTMP)]
    xsb = [A.tile([D], F32) for _ in range(2)]
    actT = A.tile([8, NT], BF16)
    smallf = A.tile([64], F32)
    stt = A.tile([NH, 6], F32)
    mvt = A.tile([NH, 2], F32)
    sTb = [A.tile([128], BF16) for _ in range(4)]
    eT = [A.tile([2, NT], BF16) for _ in range(2)]
    mark_common = A.off

    xt = A.tile([4, D], F32)
    merged = A.tile([4, D], F32)
    big = A.tile([32, NT], BF16)
    uT = A.tile([8, 30 + NT], BF16)
    v_bf = A.tile([4, D], BF16)
    S32 = A.tile([8, HD], F32)
    Sbf = A.tile([8, HD], BF16)
    kTm = A.tile([8, NMEM], BF16)
    vaug = A.tile([2, NH, HD + 1], BF16)
    cs = A.tile([2, NT], F32)

    qT = big[:, 0:8, :]
    kT = big[:, 8:16, :]
    qxT = big[:, 16:24, :]
    khat = big[:, 24:32, :].rearrange("p a b -> p (a b)").rearrange("p (c f) -> p c f", c=4)
    yv = xt.rearrange("p a b -> p (a b)").rearrange("p (a b) -> p a b", a=8)
    rT = big

    tmp_i = [0]

    tmp_pool = [tuple(range(NTMP))]

    def newtmp():
        pool = tmp_pool[0]
        i = pool[tmp_i[0] % len(pool)]
        tmp_i[0] += 1
        return tmp[i], ("tmp", i)

    bank_i = [0]

    cur_pool = [(0, 1, 2, 3, 4, 5, 6, 7)]

    def bank():
        pool = cur_pool[0]
        b = pool[bank_i[0] % len(pool)]
        bank_i[0] += 1
        return b

    def ACT(out, in_, func, r, w, scale=None, bias=None, accum=None):
        kw = {}
        if scale is not None:
            kw["scale"] = scale
        if bias is not None:
            kw["bias"] = bias
        if accum is not None:
            kw["accum_out"] = accum
        S.op("act", lambda e: e.activation(out, in_, func, **kw), r, w)

    def TT(eng, out, a, b, op, r, w):
        S.op(eng, lambda e: e.tensor_tensor(out, a, b, op), r, w)

    def STT(out, in0, scalar, in1, op0, op1, r, w):
        S.op("dve", lambda e: e.scalar_tensor_tensor(out, in0, scalar, in1, op0, op1), r, w)

    def TS(eng, out, in0, s1, s2, op0, op1, r, w):
        if op1 is None:
            S.op(eng, lambda e: e.tensor_scalar(out, in0, s1, None, op0), r, w)
        else:
            S.op(eng, lambda e: e.tensor_scalar(out, in0, s1, s2, op0, op1), r, w)

    def MM(out, lhsT, rhs, start, stop, r, w):
        S.op("pe", lambda e: e.matmul(out, lhsT, rhs, start=start, stop=stop), r, w)

    def TR(out, in_, ident, r, w):
        S.op("pe", lambda e: e.transpose(out, in_, ident), r, w)

    class WStream:
        def __init__(self, views, tok, look, shape):
            self.views, self.tok, self.look, self.shape = views, tok, look, shape
            self.n = len(views)
            self.i = 0
            self.anchors = []
            self.scr = {}

        def get(self, src, key=None):
            i = self.i
            self.i += 1
            slot = i % self.n
            dst = self.views[slot]
            store = None
            if key is not None and key in self.scr:
                sap = self.scr[key]
                op = S.make_dma("sp", lambda e: e.dma_start(out=dst, in_=sap), r=[("scr", self.tok, key)],
                                w=[(self.tok, slot)])
            else:
                op = S.make_dma("pool", lambda e: e.dma_start(out=dst, in_=src, max_dma_last_dim=2048),
                                w=[(self.tok, slot)])
                if key is not None:
                    sap = nc.dram_tensor("scr_%s_%d" % (self.tok, len(self.scr)), [128] + list(self.shape), BF16,
                                         kind="Internal").ap()
                    self.scr[key] = sap
                    store = S.make_dma("sp", lambda e: e.dma_start(out=sap, in_=dst), r=[(self.tok, slot)],
                                       w=[("scr", self.tok, key)])
            if i >= self.look:
                S.insert_after(self.anchors[i - self.look], op)
            else:
                S.ops.append(op)
            self.anchors.append(S.marker())
            if store is not None:
                S.ops.append(store)
            return slot

    W = WStream(wbuf, "w", 2, [8, 512])
    DG = WStream(dgbuf, "dg", 1, [CW, 128])

    wnames = {}

    def wblock(wap, r0, c0):
        return (wap[r0:r0 + 1024, c0:c0 + 512].rearrange("(kc p) n -> p kc n", p=128), (wap.tensor.name, r0, c0))

    col = lambda c0, n=1: cvt[:, c0:c0 + n]

    def rsqrt_cols(src, dst, n, r, w):
        lt = smallf[:, 48:48 + n]
        ACT(lt, src, AF.Ln, r, ["s_ln"])
        ACT(dst, lt, AF.Exp, ["s_ln"], w, scale=-0.5)

    def rms_A(x, xtokc, c, nparts):
        xs = xsb[c % 2][0:nparts]
        xst = ("xs", c % 2)
        ssq = smallf[0:nparts, 0:1]
        rs = smallf[0:nparts, 1:2]
        rstd = smallf[0:nparts, 2:3]
        ACT(xs, x, AF.Square, [xtokc], [xst, "s_ssq"], accum=ssq)
        TS("dve", rs, ssq, 1.0 / D, EPS, ALU.mult, ALU.add, ["s_ssq"], ["s_rs"])
        lt = smallf[0:nparts, 3:4]
        ACT(lt, rs, AF.Ln, ["s_rs"], ["s_lt"])
        ACT(rstd, lt, AF.Exp, ["s_lt"], ["s_rstd"], scale=-0.5)
        ACT(xs, x, AF.Copy, [xtokc, "s_rstd"], [xst], scale=rstd)

    def rms_B(c, nparts, gcol, dstT, dtokc):
        xs = xsb[c % 2][0:nparts]
        xst = ("xs", c % 2)
        for half in range(2):
            b = bank()
            for q in range(4):
                dc = half * 4 + q
                TR(PS[b][:, q * 128:q * 128 + nparts], xs[:, dc * 128:(dc + 1) * 128], ident_f[0:nparts, 0:nparts],
                   [xst], [("ps", b)])
            src3 = PS[b].rearrange("p (a b) -> p a b", a=4)[:, :, 0:nparts]
            g3 = cvt[:, gcol + half * 4:gcol + half * 4 + 4].unsqueeze(2).broadcast_to([128, 4, nparts])
            TT("dve", dstT[:, half * 4:half * 4 + 4, c * 128:c * 128 + nparts], src3, g3, ALU.mult,
               [("ps", b)], [dtokc])

    def rmsnorm_T(src, nparts, ntok_chunks, gcol, dstT, xtok, dtok):
        for c in range(ntok_chunks):
            rms_A(src(c), xtok(c), c, nparts)
            rms_B(c, nparts, gcol, dstT, dtok(c))

    xin = A.tile([D], F32)

    def prep_A(ti, c):
        t0 = ti * NT
        if c == 0:
            S.dma("sp", lambda e: e.dma_start(out=cs[:, 0, :], in_=cosp[:, t0:t0 + NT]), w=["cs"])
            S.dma("sp", lambda e: e.dma_start(out=cs[:, 1, :], in_=sinp[:, t0:t0 + NT]), w=["cs"])
        S.dma("act", lambda e: e.dma_start(out=xin, in_=xp[t0 + c * 128:t0 + (c + 1) * 128, :]), w=["xin"])
        rms_A(xin, "xin", c, 128)

    def prep_B(c):
        rms_B(c, 128, C_GMIX, actT, ("actT", c))

    def prep_hT(ti):
        for c in range(4):
            prep_A(ti, c)
            prep_B(c)

    for eng in ("pe", "act", "dve", "pool"):
        if eng == "pe":
            S.op("pe", lambda e: e.transpose(PS[7][:, 0:128], ident_f, ident_f), CONST, [("ps", 7)])
        elif eng == "act":
            S.op("act", lambda e: e.activation(smallf[:, 60:61], cvt[:, 0:1], AF.Copy), CONST, ["s_junk_a"])
        elif eng == "dve":
            S.op("dve", lambda e: e.tensor_copy(smallf[:, 61:62], cvt[:, 0:1]), CONST, ["s_junk_d"])
        else:
            S.op("pool", lambda e: e.tensor_copy(smallf[:, 62:63], cvt[:, 0:1]), CONST, ["s_junk_p"])

    def memkv():
        for c2 in range(2):
            S.dma("sp", lambda e, c2=c2: e.dma_start(out=xt[:, c2, :], in_=mem[c2 * 128:(c2 + 1) * 128, :]),
                  w=[("xt", c2)])
        rmsnorm_T(lambda c: xt[:, c, :], 128, 2, C_GMEM, actT, lambda c: ("xt", c), lambda c: ("actT", c))
        S.op("dve", lambda e: e.memset(vaug[:, :, :, HD:HD + 1], 1.0), w=["vaug"])
        for j in range(4):
            slot = W.get(wblock(w_kv, 0, j * 512)[0])
            wt = ("w", slot)
            for c2 in range(2):
                b = bank()
                for kc in range(8):
                    MM(PS[b], actT[:, kc, c2 * 128:(c2 + 1) * 128], wbuf[slot][:, kc, :], kc == 0, kc == 7,
                       [("actT", c2), wt], [("ps", b)])
                tp, tt = newtmp()
                ACT(tp, PS[b], AF.Copy, [("ps", b)], [tt])
                dst = (mkp if j < 2 else mvp)[c2 * 128:(c2 + 1) * 128, (j % 2) * 512:(j % 2) * 512 + 512]
                S.dma("pool", lambda e, dst=dst, tp=tp: e.dma_start(out=dst, in_=tp), r=[tt], w=[("o_kv", j, c2)])
                if j >= 2:
                    jj = j - 2
                    S.op("dve", lambda e, tp=tp, c2=c2, jj=jj: e.tensor_copy(
                        vaug[:, c2, 2 * jj:2 * jj + 2, 0:HD], tp.rearrange("p (a b) -> p a b", a=2)),
                        [tt], ["vaug"])
            if j < 2:
                for m in range(4):
                    b = bank()
                    for kc in range(8):
                        MM(PS[b][:, 0:NMEM], wbuf[slot][:, kc, m * 128:(m + 1) * 128], actT[:, kc, 0:NMEM],
                           kc == 0, kc == 7, [("actT", 0), ("actT", 1), wt], [("ps", b)])
                    S.op("dve", lambda e, b=b, j=j, m=m: e.tensor_copy(kTm[:, 4 * j + m, :], PS[b][:, 0:NMEM]),
                         [("ps", b)], ["kTm"])

    def prompt_tile(ti):
        t0 = ti * NT
        ACT_ALL = [("actT", c) for c in range(4)]
        STAGE[0] = 'S0'
        if ti == 0:
            prep_hT(0)

        def fm_block(wap, r0, c0, consumer, nm=4):
            slot = W.get(*wblock(wap, r0, c0))
            for m in range(nm):
                b = bank()
                for kc in range(8):
                    MM(PS[b], wbuf[slot][:, kc, m * 128:(m + 1) * 128], actT[:, kc, :], kc == 0, kc == 7,
                       ACT_ALL + [("w", slot)], [("ps", b)])
                consumer(m, b)

        def tm_block(wap, r0, c0, consumer):
            slot = W.get(*wblock(wap, r0, c0))
            for c in range(4):
                b = bank()
                for kc in range(8):
                    MM(PS[b], actT[:, kc, c * 128:(c + 1) * 128], wbuf[slot][:, kc, :], kc == 0, kc == 7,
                       [("actT", c), ("w", slot)], [("ps", b)])
                consumer(c, b)

        STAGE[0] = 'A_qk'
        def rope_consumer(dst, base_slab, j):
            st = {}

            def cons(m, b):
                st[m] = b
                if m % 2 == 1:
                    h = 2 * j + m // 2
                    b1, b2 = st[m - 1], st[m]
                    t1, k1 = newtmp()
                    t2, k2 = newtmp()
                    cos_, sin_ = cs[:, 0, :], cs[:, 1, :]
                    TT("dve", t1, PS[b1], cos_, ALU.mult, [("ps", b1), "cs"], [k1])
                    TT("dve", t2, PS[b2], sin_, ALU.mult, [("ps", b2), "cs"], [k2])
                    TT("pool", dst[:, 2 * h, :], t1, t2, ALU.subtract, [k1, k2], [("big", base_slab + 2 * h)])
                    t3, k3 = newtmp()
                    t4, k4 = newtmp()
                    TT("dve", t3, PS[b2], cos_, ALU.mult, [("ps", b2), "cs"], [k3])
                    TT("dve", t4, PS[b1], sin_, ALU.mult, [("ps", b1), "cs"], [k4])
                    TT("pool", dst[:, 2 * h + 1, :], t3, t4, ALU.add, [k3, k4], [("big", base_slab + 2 * h + 1)])
            return cons

        for j in range(2):
            fm_block(w_in, 0, 0 + j * 512, rope_consumer(qT, 0, j))
        for j in range(2):
            fm_block(w_in, 0, 1024 + j * 512, rope_consumer(kT, 8, j))
        STAGE[0] = 'A_vg'
        for j in range(2):
            def vcons(c, b, j=j):
                ACT(v_bf[:, c, j * 512:(j + 1) * 512], PS[b], AF.Copy, [("ps", b)], [("v", c)])
            tm_block(w_in, 0, 2048 + j * 512, vcons)
        for j in range(2):
            def gcons(c, b, j=j):
                tp, tt = newtmp()
                ACT(tp, PS[b], AF.Silu, [("ps", b)], [tt])
                TT("pool", merged[:, c, j * 512:(j + 1) * 512], tp, rowbt[:, j * 512:(j + 1) * 512], ALU.mult,
                   [tt], [("mg", c, 2 * j), ("mg", c, 2 * j + 1)])
            tm_block(w_in, 0, 3072 + j * 512, gcons)
        for j in range(2):
            def g0cons(c, b, j=j):
                tp, tt = newtmp()
                ACT(tp, PS[b], AF.Sigmoid, [("ps", b)], [tt])
                mg = merged[:, c, j * 512:(j + 1) * 512]
                TT("pool", mg, mg, tp, ALU.mult, [tt, ("mg", c, 2 * j), ("mg", c, 2 * j + 1)],
                   [("mg", c, 2 * j), ("mg", c, 2 * j + 1)])
            tm_block(w_in, 0, 7168 + j * 512, g0cons)

        STAGE[0] = 'A_ret'
        cur_pool[0] = (0, 1, 2, 3)
        def ret_stage1(c):
            cs_ = slice(c * 128, (c + 1) * 128)
            for h in range(NH):
                qk_r = [("big", 2 * h), ("big", 2 * h + 1), ("big", 8 + 2 * h), ("big", 8 + 2 * h + 1)]
                bk = bank()
                for e2 in range(2):
                    TR(PSB[bk][:, e2 * 128:(e2 + 1) * 128], kT[:, 2 * h + e2, cs_], ident_b,
                       [("big", 8 + 2 * h + e2)], [("ps", bk)])
                for e2 in range(2):
                    MM(PS[bk][:, 128:256], kT[:, 2 * h + e2, cs_], qT[:, 2 * h + e2, cs_], e2 == 0, e2 == 1,
                       qk_r, [("ps", bk)])
                ACT(khat[:, c, h * HD:(h + 1) * HD], PSB[bk][:, 0:HD], AF.Copy, [("ps", bk)], [("big", 24 + 2 * c + h // 2)],
                    scale=col(C_KDEC + h))
                sb_, sk = sTb[h], ("sTb", h)
                TT("dve", sb_, PS[bk][:, 128:256], maskt[:, h, :], ALU.mult, [("ps", bk)], [sk])

        def ret_stage2(c):
            cs_ = slice(c * 128, (c + 1) * 128)
            obanks = [4, 5]
            for h in range(NH):
                qk_r = [("big", 2 * h), ("big", 2 * h + 1), ("big", 8 + 2 * h), ("big", 8 + 2 * h + 1)]
                sb_, sk = sTb[h], ("sTb", h)
                bo = obanks[h // 2]
                oap = PS[bo][:, (h % 2) * HD:(h % 2 + 1) * HD]
                MM(oap, sb_, v_bf[:, c, h * HD:(h + 1) * HD], True, False, [sk, ("v", c)], [("ps", bo)])
                for e2 in range(2):
                    MM(oap, qT[:, 2 * h + e2, cs_], Sbf[:, 2 * h + e2, :], False, e2 == 1,
                       qk_r + [("Sbf", h)], [("ps", bo)])
                bp = bank()
                for e2 in range(2):
                    MM(PS[bp][:, e2 * HD:(e2 + 1) * HD], khat[:, c, h * HD + e2 * 128:h * HD + (e2 + 1) * 128],
                       v_bf[:, c, h * HD:(h + 1) * HD], True, True,
                       [("big", 24 + 2 * c + h // 2), ("v", c)], [("ps", bp)])
                s32v = S32[:, 2 * h:2 * h + 2, :].rearrange("p a b -> p (a b)")
                STT(s32v, s32v, GC[h], PS[bp], ALU.mult, ALU.add, [("ps", bp), ("S32", h)], [("S32", h)])
                ACT(Sbf[:, 2 * h:2 * h + 2, :].rearrange("p a b -> p (a b)"), s32v, AF.Copy, [("S32", h)], [("Sbf", h)])

        def ret_gn(c):
            obanks = [4, 5]
            for h in range(NH):
                bo = obanks[h // 2]
                S.op("dve", lambda e, h=h, bo=bo: e.bn_stats(stt[:, h, :], PS[bo][:, (h % 2) * HD:(h % 2 + 1) * HD]),
                     [("ps", bo)], ["stt"])
                S.op("dve", lambda e, h=h: e.bn_aggr(mvt[:, h, :], stt[:, h, :]), ["stt"], ["mvt"])
            vq = smallf[:, 8:12]
            TT("dve", vq, mvt[:, :, 1], col(C_QD2, 4), ALU.mult, ["mvt"], ["s_vq"])
            TS("dve", vq, vq, EPS, None, ALU.add, None, ["s_vq"], ["s_vq"])
            rs4 = smallf[:, 12:16]
            rsqrt_cols(vq, rs4, 4, ["s_vq"], ["s_rs4"])
            s2 = smallf[:, 16:20]
            TT("dve", s2, rs4, col(C_QD, 4), ALU.mult, ["s_rs4"], ["s_s2"])
            for h in range(NH):
                bo = obanks[h // 2]
                tp, tt = newtmp()
                mg = merged[:, c, h * HD:(h + 1) * HD]
                ACT(tp[:, 0:HD], mg, AF.Copy, [("mg", c, h), "s_s2"], [tt], scale=s2[:, h:h + 1])
                STT(mg, PS[bo][:, (h % 2) * HD:(h % 2 + 1) * HD], mvt[:, h, 0:1], tp[:, 0:HD], ALU.subtract, ALU.mult,
                    [("ps", bo), "mvt", tt], [("mg", c, h)])

        if ti == 0:
            dbg_dump('m1', merged, [("mg", c, q) for c in range(4) for q in range(4)])
        STAGE[0] = 'B_conv'
        if ti == 0:
            S.op("dve", lambda e: e.memset(uT[:, :, 0:30], 0.0), w=[("uTh",)])
        else:
            S.op("act", lambda e: e.activation(uT[:, :, 0:30], uT[:, :, NT:NT + 30], AF.Copy),
                 [("uT", cc) for cc in range(8)], [("uTh",)])
        cur_pool[0] = (0, 1, 2, 3, 4, 5)
        BSUM, BSQ = 6, 7
        slots = {}

        def glu_stage(cc):
            j, m = cc // 4, cc % 4
            if m == 0:
                slots[j] = (W.get(*wblock(w_in, 0, 4096 + j * 512)), W.get(*wblock(w_in, 0, 5120 + j * 512)))
            sa, sg = slots[j]
            ba, bg = bank(), bank()
            for (bb, sl) in ((ba, sa), (bg, sg)):
                for kc in range(8):
                    MM(PS[bb], wbuf[sl][:, kc, m * 128:(m + 1) * 128], actT[:, kc, :], kc == 0, kc == 7,
                       ACT_ALL + [("w", sl)], [("ps", bb)])
            tp, tt = newtmp()
            ACT(tp, PS[bg], AF.Sigmoid, [("ps", bg)], [tt])
            TT("dve", uT[:, cc, 30:30 + NT], PS[ba], tp, ALU.mult, [("ps", ba), tt], [("uT", cc)])

        def conv_stage(cc):
            ds_ = DG.get(dgw[cc], ('dg', cc))
            by = bank()
            for jt in range(CW):
                MM(PS[by], dgbuf[ds_][:, jt, :], uT[:, cc, jt:jt + NT], jt == 0, jt == CW - 1,
                   [("uT", cc), ("uTh",), ("dg", ds_)], [("ps", by)])
            ytok = [("xt", cc // 2)]
            ACT(yv[:, cc, :], PS[by], AF.Identity, [("ps", by)], ytok, bias=col(C_CB + cc))

        def conv_stats(cc):
            ytok = [("xt", cc // 2)]
            tq, tqt = newtmp()
            ACT(tq, yv[:, cc, :], AF.Square, ytok, [tqt])
            MM(PS[BSUM], ones_f, yv[:, cc, :], cc == 0, cc == 7, ytok + ["c_ones"], [("ps", BSUM)])
            MM(PS[BSQ], ones_f, tq, cc == 0, cc == 7, [tqt], [("ps", BSQ)])

        RPOOL, CPOOL = (0, 1, 2), (3, 6, 7)
        RTMP, CTMP = (0, 1, 2), (3, 4, 5)

        def RR(f, *a):
            cur_pool[0] = RPOOL
            tmp_pool[0] = RTMP
            f(*a)

        def CC(f, *a):
            cur_pool[0] = CPOOL
            tmp_pool[0] = CTMP
            f(*a)

        RR(ret_stage1, 0)
        RR(ret_stage2, 0)
        CC(glu_stage, 0)
        for c in range(1, 4):
            RR(ret_gn, c - 1)
            for cc in (2 * (c - 1), 2 * (c - 1) + 1):
                CC(glu_stage, cc + 1)
                CC(conv_stage, cc)
            RR(ret_stage1, c)
            RR(ret_stage2, c)
        RR(ret_gn, 3)
        for cc in (6, 7):
            if cc + 1 < 8:
                CC(glu_stage, cc + 1)
            CC(conv_stage, cc)
        tmp_pool[0] = tuple(range(NTMP))
        if ti == NTILES - 1:
            S.dma("pool", lambda e: e.dma_start(out=srp.rearrange("h (e p) v -> p (h e) v", p=128), in_=S32),
                  r=[("S32", h) for h in range(NH)], w=["o_srp"])

        cur_pool[0] = (0, 1, 2, 3, 4, 5)
        for cc in range(8):
            conv_stats(cc)
        if ti == NTILES - 1:
            for j in range(2):
                sa = W.get(*wblock(w_in, 0, 4096 + j * 512))
                sg = W.get(*wblock(w_in, 0, 5120 + j * 512))
                ba, bg = bank(), bank()
                for (bb, sl) in ((ba, sa), (bg, sg)):
                    for kc in range(8):
                        MM(PS[bb], actT[:, kc, 3 * 128:4 * 128], wbuf[sl][:, kc, :], kc == 0, kc == 7,
                           [("actT", 3), ("w", sl)], [("ps", bb)])
                tp, tt = newtmp()
                ACT(tp, PS[bg], AF.Sigmoid, [("ps", bg)], [tt])
                tu, tut = newtmp()
                TT("dve", tu, PS[ba], tp, ALU.mult, [("ps", ba), tt], [tut])
                S.dma("pool", lambda e, tu=tu, j=j: e.dma_start(out=scp[:, j * 512:(j + 1) * 512], in_=tu[98:128, :]),
                      r=[tut], w=[("o_scp", j)])
        STAGE[0] = 'B_ln'
        mean, mk_ = newtmp()
        msq, qk_ = newtmp()
        lnv, lk_ = newtmp()
        ACT(mean, PS[BSUM], AF.Copy, [("ps", BSUM)], [mk_], scale=1.0 / D)
        ACT(msq, PS[BSUM], AF.Square, [("ps", BSUM)], [qk_], scale=1.0 / D)
        STT(msq, PS[BSQ], 1.0 / D, msq, ALU.mult, ALU.subtract, [("ps", BSQ), qk_], [qk_])
        TS("dve", msq, msq, EPS, None, ALU.add, None, [qk_], [qk_])
        ACT(lnv, msq, AF.Ln, [qk_], [lk_])
        ACT(PS[BSUM], lnv, AF.Exp, [lk_], [("ps", BSUM)], scale=-0.5)
        STT(PS[BSQ], mean, -1.0, PS[BSUM], ALU.mult, ALU.mult, [mk_, ("ps", BSUM)], [("ps", BSQ)])
        zT = v_bf.rearrange("p a b -> p (a b)").rearrange("p (a b) -> p a b", a=8)

        def ln_apply(cc):
            ytok = [("xt", cc // 2)]
            TT("dve", yv[:, cc, :], yv[:, cc, :], PS[BSUM], ALU.mult, ytok + [("ps", BSUM)], ytok)
            TT("dve", yv[:, cc, :], yv[:, cc, :], PS[BSQ], ALU.add, ytok + [("ps", BSQ)], ytok)

        def ln_silu(cc):
            ytok = [("xt", cc // 2)]
            ACT(zT[:, cc, :], yv[:, cc, :], AF.Silu, ytok, [("v", cc // 2)],
                scale=col(C_LNG + cc), bias=col(C_LNB + cc))

        STAGE[0] = 'C_xa'
        for j in range(2):
            def qxcons(m, b, j=j):
                ACT(qxT[:, 4 * j + m, :], PS[b], AF.Copy, [("ps", b)], [("big", 16 + 4 * j + m)])
            fm_block(w_in, 0, 6144 + j * 512, qxcons)
            for hh in range(2):
                h = 2 * j + hh
                for nch in range(2):
                    bx = bank()
                    for e2 in range(2):
                        MM(PS[bx], kTm[:, 2 * h + e2, nch * 128:(nch + 1) * 128], qxT[:, 2 * h + e2, :],
                           e2 == 0, e2 == 1, ["kTm", ("big", 16 + 2 * h), ("big", 16 + 2 * h + 1)], [("ps", bx)])
                    ACT(eT[hh][:, nch, :], PS[bx], AF.Exp, [("ps", bx)], [("eT", hh)], scale=1.0 / 16.0)
            slot = W.get(*wblock(w_in, 0, 9216 + j * 512))
            for c in range(4):
                bg = bank()
                for kc in range(8):
                    MM(PS[bg], actT[:, kc, c * 128:(c + 1) * 128], wbuf[slot][:, kc, :], kc == 0, kc == 7,
                       [("actT", c), ("w", slot)], [("ps", bg)])
                tp, tt = newtmp()
                ACT(tp, PS[bg], AF.Sigmoid, [("ps", bg)], [tt])
                for hh in range(2):
                    h = 2 * j + hh
                    bo = bank()
                    for nch in range(2):
                        MM(PS[bo][:, 0:HD + 1], eT[hh][:, nch, c * 128:(c + 1) * 128], vaug[:, nch, h, :],
                           nch == 0, nch == 1, [("eT", hh), "vaug"], [("ps", bo)])
                    rden = smallf[:, 20 + hh:21 + hh]
                    S.op("dve", lambda e, rden=rden, bo=bo: e.reciprocal(rden, PS[bo][:, HD:HD + 1]),
                         [("ps", bo)], [("s_rden", hh)])
                    tq, tqt = newtmp()
                    STT(tq[:, 0:HD], PS[bo][:, 0:HD], rden, tp[:, hh * HD:(hh + 1) * HD], ALU.mult, ALU.mult,
                        [("ps", bo), ("s_rden", hh), tt], [tqt])
                    mg = merged[:, c, h * HD:(h + 1) * HD]
                    TT("pool", mg, mg, tq[:, 0:HD], ALU.add, [tqt, ("mg", c, h)], [("mg", c, h)])
                ln_apply(4 * j + c)
            for cc in range(4 * j, 4 * j + 4):
                ln_silu(cc)

        if ti == 0:
            dbg_dump('m2', merged, [("mg", c, q) for c in range(4) for q in range(4)])
        STAGE[0] = 'B_out'
        for c in range(4):
            S.dma("sp", lambda e, c=c: e.dma_start(out=xt[:, c, :], in_=xp[t0 + c * 128:t0 + (c + 1) * 128, :]),
                  w=[("xt", c)])
        mbufs = {}

        def mergeA(c):
            tp, tt = newtmp()
            mbv = tp.bitcast(BF16)
            ACT(mbv, merged[:, c, :], AF.Copy, [("mg", c, q) for q in range(4)], [tt])
            mbufs[c] = (mbv, tt)

        def mergeB(c):
            mbv, tt = mbufs[c]
            b = bank()
            for dc in range(8):
                TR(PSB[b][:, dc * 128:(dc + 1) * 128], mbv[:, dc * 128:(dc + 1) * 128], ident_b, [tt], [("ps", b)])
            S.op("dve", lambda e, b=b, c=c: e.tensor_copy(actT[:, :, c * 128:(c + 1) * 128],
                                                         PSB[b].rearrange("p (a b) -> p a b", a=8)),
                 [("ps", b)], [("actT", c)])

        for j in range(2):
            sq_ = W.get(*wblock(w_in, 0, 8192 + j * 512))
            sc_ = W.get(*wblock(w_co, 0, j * 512))
            for c in range(4):
                b1, b2 = bank(), bank()
                for kc in range(8):
                    MM(PS[b1], actT[:, kc, c * 128:(c + 1) * 128], wbuf[sq_][:, kc, :], kc == 0, kc == 7,
                       [("actT", c), ("w", sq_)], [("ps", b1)])
                for kc in range(8):
                    MM(PS[b2], zT[:, kc, c * 128:(c + 1) * 128], wbuf[sc_][:, kc, :], kc == 0, kc == 7,
                       [("v", kc // 2), ("w", sc_)], [("ps", b2)])
                if j == 1 and c >= 1:
                    STAGE[0] = 'merge_out'
                    mergeB(c - 1)
                    STAGE[0] = 'B_out'
                tp, tt = newtmp()
                ACT(tp, PS[b1], AF.Sigmoid, [("ps", b1)], [tt])
                tq, tqt = newtmp()
                TT("dve", tq, PS[b2], tp, ALU.mult, [("ps", b2), tt], [tqt])
                mg = merged[:, c, j * 512:(j + 1) * 512]
                mt = [("mg", c, 2 * j), ("mg", c, 2 * j + 1)]
                TT("pool", mg, mg, tq, ALU.add, [tqt] + mt, mt)
                if j == 1:
                    mergeA(c)

        STAGE[0] = 'merge_out'
        mergeB(3)
        cur_pool[0] = (0, 1, 2, 3, 4, 5, 6, 7)
        so = [W.get(*wblock(w_out, 0, j * 512)) for j in range(2)]

        def outproj(c):
            for j in range(2):
                b = bank()
                for kc in range(8):
                    MM(PS[b], actT[:, kc, c * 128:(c + 1) * 128], wbuf[so[j]][:, kc, :], kc == 0, kc == 7,
                       [("actT", c), ("w", so[j])], [("ps", b)])
                xv = xt[:, c, j * 512:(j + 1) * 512]
                TT("dve", xv, PS[b], xv, ALU.add, [("ps", b), ("xt", c)], [("xt", c)])

        if ti == 0:
            pass
        STAGE[0] = 'merge_out'
        outproj(0)
        rms_A(xt[:, 0, :], ("xt", 0), 0, 128)
        for c in range(1, 4):
            outproj(c)
            rms_B(c - 1, 128, C_GFFN, actT, ("actT", c - 1))
            rms_A(xt[:, c, :], ("xt", c), c, 128)
        rms_B(3, 128, C_GFFN, actT, ("actT", 3))
        if ti == 0:
            dbg_dump('x2', xt, [('xt', c) for c in range(4)])
        STAGE[0] = 'ffn_up'
        for j in range(8):
            def ucons(m, b, j=j):
                f = 4 * j + m
                tp, tt = newtmp()
                ACT(tp, PS[b], AF.Square, [("ps", b)], [tt])
                STT(rT[:, f, :], PS[b], 0.0, tp, ALU.is_gt, ALU.mult, [("ps", b), tt], [("big", f)])
            fm_block(w_up, 0, j * 512, ucons)
        if ti + 1 < ntiles:
            STAGE[0] = 'prep'
            prep_A(ti + 1, 0)
            prep_A(ti + 1, 1)
        STAGE[0] = 'ffn_down'
        for ch in range(2):
            bks = [4 * ch + c for c in range(4)]
            for fg in range(4):
                slot = W.get(*wblock(w_dn, fg * 1024, ch * 512))
                for c in range(4):
                    for fc in range(8):
                        MM(PS[bks[c]], rT[:, fg * 8 + fc, c * 128:(c + 1) * 128], wbuf[slot][:, fc, :],
                           fg == 0 and fc == 0, fg == 3 and fc == 7, [("big", fg * 8 + fc), ("w", slot)],
                           [("ps", bks[c])])
            for c in range(4):
                xv = xt[:, c, ch * 512:(ch + 1) * 512]
                TT("dve", xv, PS[bks[c]], xv, ALU.add, [("ps", bks[c]), ("xt", c)], [("xt", c)])
            if ti + 1 < ntiles:
                STAGE[0] = 'prep'
                if ch == 0:
                    prep_B(0)
                    prep_B(1)
                    prep_A(ti + 1, 2)
                    prep_A(ti + 1, 3)
                else:
                    prep_B(2)
                    prep_B(3)
                STAGE[0] = 'ffn_down'
        STAGE[0] = 'final'
        for c in range(4):
            x = xt[:, c, :]
            yo = xsb[c % 2]
            ssq = smallf[:, 0:1]
            rs = smallf[:, 1:2]
            lt = smallf[:, 3:4]
            rstd = smallf[:, 2:3]
            ACT(yo, x, AF.Square, [("xt", c)], [("xs", c % 2), "s_ssq"], accum=ssq)
            TS("dve", rs, ssq, 1.0 / D, EPS, ALU.mult, ALU.add, ["s_ssq"], ["s_rs"])
            ACT(lt, rs, AF.Ln, ["s_rs"], ["s_lt"])
            ACT(rstd, lt, AF.Exp, ["s_lt"], ["s_rstd"], scale=-0.5)
            STT(yo, x, rstd, rowbt[:, D:2 * D], ALU.mult, ALU.mult, [("xt", c), "s_rstd"], [("xs", c % 2)])
            S.dma("pool", lambda e, yo=yo, c=c: e.dma_start(out=yp[t0 + c * 128:t0 + (c + 1) * 128, :], in_=yo),
                  r=[("xs", c % 2)], w=[("o_yp", ti, c)])

    def sample_pass():
        STAGE[0] = 'sample'
        S.barrier()
        A.off = mark_common
        NP = NS
        xts = A.tile([D], F32)
        mgs = A.tile([D], F32)
        bigs = A.tile([32, NS], BF16)
        v_s = A.tile([D], BF16)
        khat_s = A.tile([D], BF16)
        S32s = [A.tile([8, HD], F32) for _ in range(3)]
        Sbfs = [A.tile([8, HD], BF16) for _ in range(2)]
        Kb = [A.tile([2, D], BF16) for _ in range(2)]
        vaugs = [A.tile([2, NH, HD + 1], BF16) for _ in range(2)]
        kTs_ = [A.tile([8, NMEM], BF16) for _ in range(2)]
        ext = A.tile([8, SB, 34], BF16)
        stg = [A.tile([D], F32) for _ in range(2)]
        khs = [A.tile([D], BF16) for _ in range(2)]
        qpad = [A.tile([8, NS], BF16) for _ in range(2)]
        eTp = [A.tile([2, NH, NS], BF16) for _ in range(2)]
        sTs = A.tile([NS], BF16)
        yvs = A.tile([8, NS], F32)
        qTs, kTs = bigs[:, 0:8, :], bigs[:, 8:16, :]
        qxTs, zTs, rTs = bigs[:, 16:24, :], bigs[:, 24:32, :], bigs
        hTs = actT[:, :, 0:NS]
        HT = [("actT", 0)]
        cos_, sin_ = cst_s[:, 0, :], cst_s[:, 1, :]
        GEN = (0, 1, 2, 3)

        cur_pool[0] = (0, 1, 2, 3)
        S.dma("sp", lambda e: e.dma_start(out=xts[0:NP], in_=xsm), w=[("xt", 0)])
        rmsnorm_T(lambda c: xts[0:NP], NP, 1, C_GMIX, actT, lambda c: ("xt", 0), lambda c: ("actT", 0))

        def fm_block(wap, r0, c0, consumer):
            slot = W.get(*wblock(wap, r0, c0))
            for m in range(4):
                b = bank()
                for kc in range(8):
                    MM(PS[b][:, 0:NS], wbuf[slot][:, kc, m * 128:(m + 1) * 128], hTs[:, kc, :], kc == 0, kc == 7,
                       HT + [("w", slot)], [("ps", b)])
                consumer(m, b)

        def tm_block(wap, r0, c0, consumer):
            slot = W.get(*wblock(wap, r0, c0))
            b = bank()
            for kc in range(8):
                MM(PS[b][0:NP], hTs[:, kc, :], wbuf[slot][:, kc, :], kc == 0, kc == 7, HT + [("w", slot)], [("ps", b)])
            consumer(b)

        def rope_consumer(dst, base_slab, j):
            st = {}

            def cons(m, b):
                st[m] = b
                if m % 2 == 1:
                    h = 2 * j + m // 2
                    b1, b2 = st[m - 1], st[m]
                    t1, k1 = newtmp()
                    t2, k2 = newtmp()
                    p1, p2 = PS[b1][:, 0:NS], PS[b2][:, 0:NS]
                    TT("dve", t1[:, 0:NS], p1, cos_, ALU.mult, [("ps", b1)], [k1])
                    TT("dve", t2[:, 0:NS], p2, sin_, ALU.mult, [("ps", b2)], [k2])
                    TT("dve", dst[:, 2 * h, :], t1[:, 0:NS], t2[:, 0:NS], ALU.subtract, [k1, k2], [("big", base_slab + 2 * h)])
                    t3, k3 = newtmp()
                    t4, k4 = newtmp()
                    TT("dve", t3[:, 0:NS], p2, cos_, ALU.mult, [("ps", b2)], [k3])
                    TT("dve", t4[:, 0:NS], p1, sin_, ALU.mult, [("ps", b1)], [k4])
                    TT("dve", dst[:, 2 * h + 1, :], t3[:, 0:NS], t4[:, 0:NS], ALU.add, [k3, k4], [("big", base_slab + 2 * h + 1)])
            return cons

        for j in range(2):
            fm_block(w_in, 0, j * 512, rope_consumer(qTs, 0, j))
        for j in range(2):
            fm_block(w_in, 0, 1024 + j * 512, rope_consumer(kTs, 8, j))
        for j in range(2):
            def vcons(b, j=j):
                ACT(v_s[0:NP, j * 512:(j + 1) * 512], PS[b][0:NP], AF.Copy, [("ps", b)], [("v", 0)])
            tm_block(w_in, 0, 2048 + j * 512, vcons)
        MG = lambda q: ("mg", 0, q)
        for j in range(2):
            def gcons(b, j=j):
                tp, tt = newtmp()
                ACT(tp[0:NP], PS[b][0:NP], AF.Silu, [("ps", b)], [tt])
                TT("pool", mgs[0:NP, j * 512:(j + 1) * 512], tp[0:NP], rowbt[0:NP, j * 512:(j + 1) * 512], ALU.mult,
                   [tt], [MG(2 * j), MG(2 * j + 1)])
            tm_block(w_in, 0, 3072 + j * 512, gcons)
        for j in range(2):
            def g0cons(b, j=j):
                tp, tt = newtmp()
                ACT(tp[0:NP], PS[b][0:NP], AF.Sigmoid, [("ps", b)], [tt])
                mg = mgs[0:NP, j * 512:(j + 1) * 512]
                TT("pool", mg, mg, tp[0:NP], ALU.mult, [tt, MG(2 * j), MG(2 * j + 1)], [MG(2 * j), MG(2 * j + 1)])
            tm_block(w_in, 0, 7168 + j * 512, g0cons)

        STAGE[0] = 's_ret'
        OB = (4, 5, 6, 7)
        for h in range(NH):
            bk = bank()
            for e2 in range(2):
                TR(PSB[bk][0:NP, e2 * 128:(e2 + 1) * 128], kTs[:, 2 * h + e2, :], ident_b, [("big", 8 + 2 * h + e2)], [("ps", bk)])
            ACT(khat_s[0:NP, h * HD:(h + 1) * HD], PSB[bk][0:NP, 0:HD], AF.Copy, [("ps", bk)], [("khat", h)],
                scale=cvt[0:NP, C_SKDEC + h:C_SKDEC + h + 1])
            bs = bank()
            for e2 in range(2):
                MM(PS[bs][0:NP, 0:NS], kTs[:, 2 * h + e2, :], qTs[:, 2 * h + e2, :], e2 == 0, e2 == 1,
                   [("big", 2 * h), ("big", 2 * h + 1), ("big", 8 + 2 * h), ("big", 8 + 2 * h + 1)], [("ps", bs)])
            TT("dve", sTs[0:NP], PS[bs][0:NP, 0:NS], maskst[0:NP, h, :], ALU.mult, [("ps", bs)], ["sTs"])
            MM(PS[OB[h]][0:NP, 0:HD], sTs[0:NP], v_s[0:NP, h * HD:(h + 1) * HD], True, False, ["sTs", ("v", 0)], [("ps", OB[h])])
        def load_state(s):
            p3 = s % 3
            S.dma("sp", lambda e: e.dma_start(out=S32s[p3], in_=sret[s].rearrange("h (e p) v -> p (h e) v", p=128)),
                  w=[("S32s", p3)])
        def ret_prep(s):
            par = s % 2
            p3 = s % 3
            ACT(Sbfs[par].rearrange("p a b -> p (a b)"), S32s[p3].rearrange("p a b -> p (a b)"), AF.Copy,
                [("S32s", p3)], [("Sbfs", par)])
            S.op("dve", lambda e, s=s, par=par: e.tensor_tensor(
                qpad[par], qTs, selt[:, s, :].unsqueeze(1).broadcast_to([128, 8, NS]), ALU.mult),
                [("big", q) for q in range(8)], [("qpad", par)])
            ACT(khs[par][0:NP], khat_s[0:NP], AF.Copy, [("khat", h) for h in range(NH)], [("khs", par)],
                scale=cvt[0:NP, C_SEL + s:C_SEL + s + 1])

        def ret_main(s):
            par = s % 2
            p3 = s % 3
            for h in range(NH):
                for e2 in range(2):
                    MM(PS[OB[h]][0:NP, 0:HD], qpad[par][:, 2 * h + e2, :], Sbfs[par][:, 2 * h + e2, :], False,
                       (s == SB - 1 and e2 == 1), [("qpad", par), ("Sbfs", par)], [("ps", OB[h])])
            for h in range(NH):
                bp = bank()
                for e2 in range(2):
                    MM(PS[bp][:, e2 * HD:(e2 + 1) * HD], khs[par][0:NP, h * HD + e2 * 128:h * HD + (e2 + 1) * 128],
                       v_s[0:NP, h * HD:(h + 1) * HD], True, True, [("khs", par), ("v", 0)], [("ps", bp)])
                s32v = S32s[p3][:, 2 * h:2 * h + 2, :].rearrange("p a b -> p (a b)")
                STT(s32v, s32v, G4[h], PS[bp], ALU.mult, ALU.add, [("ps", bp), ("S32s", p3)], [("S32s", p3)])
            S.dma("pool", lambda e, s=s, p3=p3: e.dma_start(out=srs[s].rearrange("h (e p) v -> p (h e) v", p=128), in_=S32s[p3]),
                  r=[("S32s", p3)], w=[("o_srs", s)])

        load_state(0)
        load_state(1)
        ret_prep(0)
        for s in range(SB):
            if s + 2 < SB:
                load_state(s + 2)
            if s + 1 < SB:
                ret_prep(s + 1)
            ret_main(s)
        STAGE[0] = 's_gn'
        for h in range(NH):
            S.op("dve", lambda e, h=h: e.bn_stats(stt[0:NP, h, :], PS[OB[h]][0:NP, 0:HD]), [("ps", OB[h])], ["stt"])
            S.op("dve", lambda e, h=h: e.bn_aggr(mvt[0:NP, h, :], stt[0:NP, h, :]), ["stt"], ["mvt"])
        vq = smallf[0:NP, 8:12]
        TT("dve", vq, mvt[0:NP, :, 1], cvt[0:NP, C_SQD2:C_SQD2 + 4], ALU.mult, ["mvt"], ["s_vq"])
        TS("dve", vq, vq, EPS, None, ALU.add, None, ["s_vq"], ["s_vq"])
        rs4 = smallf[0:NP, 12:16]
        lt4 = smallf[0:NP, 48:52]
        ACT(lt4, vq, AF.Ln, ["s_vq"], ["s_ln"])
        ACT(rs4, lt4, AF.Exp, ["s_ln"], ["s_rs4"], scale=-0.5)
        s2 = smallf[0:NP, 16:20]
        TT("dve", s2, rs4, cvt[0:NP, C_SQD:C_SQD + 4], ALU.mult, ["s_rs4"], ["s_s2"])
        for h in range(NH):
            tp, tt = newtmp()
            mg = mgs[0:NP, h * HD:(h + 1) * HD]
            ACT(tp[0:NP, 0:HD], mg, AF.Copy, [MG(h), "s_s2"], [tt], scale=s2[:, h:h + 1])
            STT(mg, PS[OB[h]][0:NP, 0:HD], mvt[0:NP, h, 0:1], tp[0:NP, 0:HD], ALU.subtract, ALU.mult,
                [("ps", OB[h]), "mvt", tt], [MG(h)])

        STAGE[0] = 's_xa'
        for j in range(2):
            def qxcons(m, b, j=j):
                ACT(qxTs[:, 4 * j + m, :], PS[b][:, 0:NS], AF.Copy, [("ps", b)], [("big", 16 + 4 * j + m)])
            fm_block(w_in, 0, 6144 + j * 512, qxcons)
        for par in range(2):
            S.op("dve", lambda e, par=par: e.memset(vaugs[par][:, :, :, HD:HD + 1], 1.0), w=[("vaugs", par)])
        QX = [("big", 16 + q) for q in range(8)]
        kst = stg
        vst = [A.tile([D], F32) for _ in range(2)]

        def dma_k(s):
            for nch in range(2):
                S.dma("sp", lambda e, nch=nch: e.dma_start(out=kst[nch], in_=ck[s, nch * 128:(nch + 1) * 128, :]),
                      w=[("stg", nch)])

        def cast_k(s):
            par = s % 2
            for nch in range(2):
                ACT(Kb[par][:, nch, :], kst[nch], AF.Copy, [("stg", nch)], [("Kb", par)])

        def dma_v(s):
            for nch in range(2):
                S.dma("sp", lambda e, nch=nch: e.dma_start(out=vst[nch], in_=cv[s, nch * 128:(nch + 1) * 128, :]),
                      w=[("vst", nch)])

        def cast_v(s):
            par = s % 2
            S.op("dve", lambda e, par=par: e.tensor_copy(
                vaugs[par][:, 0, :, 0:HD], vst[0].rearrange("p (h d) -> p h d", h=NH)),
                [("vst", 0)], [("vaugs", par)])
            ACT(vaugs[par][:, 1, :, 0:HD], vst[1].rearrange("p (h d) -> p h d", h=NH), AF.Copy,
                [("vst", 1)], [("vaugs", par)])

        def xa_A(s):
            par = s % 2
            for nch in range(2):
                bt = bank()
                for dc in range(8):
                    TR(PSB[bt][:, dc * 128:(dc + 1) * 128], Kb[par][:, nch, dc * 128:(dc + 1) * 128], ident_b,
                       [("Kb", par)], [("ps", bt)])
                S.op("dve", lambda e, bt=bt, par=par, nch=nch: e.tensor_copy(
                    kTs_[par][:, :, nch * 128:(nch + 1) * 128], PSB[bt].rearrange("p (a b) -> p a b", a=8)),
                    [("ps", bt)], [("kTs_", par)])

        def xa_B1(s):
            par = s % 2
            bx = bank()
            for nch in range(2):
                for h in range(NH):
                    for e2 in range(2):
                        MM(PS[bx][:, (nch * NH + h) * DS:(nch * NH + h + 1) * DS],
                           kTs_[par][:, 2 * h + e2, nch * 128:(nch + 1) * 128], qxTs[:, 2 * h + e2, s * DS:(s + 1) * DS],
                           e2 == 0, e2 == 1, [("kTs_", par)] + QX, [("ps", bx)])
            S.op("dve", lambda e, par=par: e.memset(eTp[par], 0.0), w=[("eTp", par)])
            S.op("act", lambda e, s=s, par=par, bx=bx: e.activation(
                eTp[par][:, :, :, s * DS:(s + 1) * DS], PS[bx][:, 0:2 * NH * DS].rearrange("p (a b c) -> p a b c", a=2, b=NH),
                AF.Exp, scale=1.0 / 16.0), [("ps", bx), ("eTp", par)], [("eTp", par)])

        def xa_B2(s):
            par = s % 2
            for h in range(NH):
                for nch in range(2):
                    MM(PS[OB[h]][0:NP, 0:HD + 1], eTp[par][:, nch, h, :], vaugs[par][:, nch, h, :],
                       (s == 0 and nch == 0), (s == SB - 1 and nch == 1), [("eTp", par), ("vaugs", par)], [("ps", OB[h])])

        for s0 in range(2):
            dma_k(s0)
            cast_k(s0)
            dma_v(s0)
            cast_v(s0)
        xa_A(0)
        for s in range(SB):
            if s + 2 < SB:
                dma_k(s + 2)
            xa_B1(s)
            if s + 1 < SB:
                xa_A(s + 1)
            xa_B2(s)
            if s + 2 < SB:
                cast_k(s + 2)
                dma_v(s + 2)
                cast_v(s + 2)
        for j in range(2):
            def g2cons(b, j=j):
                tp, tt = newtmp()
                ACT(tp[0:NP], PS[b][0:NP], AF.Sigmoid, [("ps", b)], [tt])
                for hh in range(2):
                    h = 2 * j + hh
                    rden = smallf[0:NP, 20 + hh:21 + hh]
                    S.op("dve", lambda e, rden=rden, h=h: e.reciprocal(rden, PS[OB[h]][0:NP, HD:HD + 1]),
                         [("ps", OB[h])], [("s_rden", hh)])
                    tq, tqt = newtmp()
                    STT(tq[0:NP, 0:HD], PS[OB[h]][0:NP, 0:HD], rden, tp[0:NP, hh * HD:(hh + 1) * HD], ALU.mult, ALU.mult,
                        [("ps", OB[h]), ("s_rden", hh), tt], [tqt])
                    mg = mgs[0:NP, h * HD:(h + 1) * HD]
                    TT("pool", mg, mg, tq[0:NP, 0:HD], ALU.add, [tqt, MG(h)], [MG(h)])
            tm_block(w_in, 0, 9216 + j * 512, g2cons)

        STAGE[0] = 's_conv'
        for s in range(SB):
            S.dma("sp", lambda e, s=s: e.dma_start(out=scs[s, 0:26, :], in_=sconv[s, 4:30, :]), w=[("o_scs_h", s)])
        for g in range(4):
            par = g % 2
            S.dma("sp", lambda e, g=g, par=par: e.dma_start(
                out=stg[par][0:120], in_=sconv[4 * g:4 * g + 4].rearrange("s r f -> (s r) f")), w=[("stg", par)])
            tp, tt = newtmp()
            sb16 = tp.bitcast(BF16)
            ACT(sb16[0:120], stg[par][0:120], AF.Copy, [("stg", par)], [tt])
            bt = bank()
            for cc in range(8):
                TR(PSB[bt][:, cc * 128:cc * 128 + 120], sb16[0:120, cc * 128:(cc + 1) * 128], ident_b[0:120, 0:120],
                   [tt], [("ps", bt)])
            for cc in range(8):
                S.op("dve", lambda e, bt=bt, cc=cc, g=g: e.tensor_copy(
                    ext[:, cc, 4 * g:4 * g + 4, 0:30], PSB[bt][:, cc * 128:cc * 128 + 120].rearrange("p (a b) -> p a b", a=4)),
                    [("ps", bt)], [("ext", cc)])
        BSUM, BSQ = 4, 5
        for j in range(2):
            sa = W.get(*wblock(w_in, 0, 4096 + j * 512))
            sg = W.get(*wblock(w_in, 0, 5120 + j * 512))
            ba, bg = bank(), bank()
            for (bb, sl) in ((ba, sa), (bg, sg)):
                for kc in range(8):
                    MM(PS[bb][0:NP], hTs[:, kc, :], wbuf[sl][:, kc, :], kc == 0, kc == 7, HT + [("w", sl)], [("ps", bb)])
            tp, tt = newtmp()
            ACT(tp[0:NP], PS[bg][0:NP], AF.Sigmoid, [("ps", bg)], [tt])
            tu, tut = newtmp()
            TT("dve", tu[0:NP], PS[ba][0:NP], tp[0:NP], ALU.mult, [("ps", ba), tt], [tut])
            for s in range(SB):
                S.dma("pool", lambda e, tu=tu, j=j, s=s: e.dma_start(
                    out=scs[s, 26:30, j * 512:(j + 1) * 512], in_=tu[s * DS:(s + 1) * DS, :]),
                    r=[tut], w=[("o_scs_u", s, j)])
            for m in range(4):
                cc = 4 * j + m
                ba, bg = bank(), bank()
                for (bb, sl) in ((ba, sa), (bg, sg)):
                    for kc in range(8):
                        MM(PS[bb][:, 0:NS], wbuf[sl][:, kc, m * 128:(m + 1) * 128], hTs[:, kc, :], kc == 0, kc == 7,
                           HT + [("w", sl)], [("ps", bb)])
                tp, tt = newtmp()
                ACT(tp[:, 0:NS], PS[bg][:, 0:NS], AF.Sigmoid, [("ps", bg)], [tt])
                S.op("dve", lambda e, cc=cc, ba=ba, tp=tp: e.tensor_tensor(
                    ext[:, cc, :, 30:34], PS[ba][:, 0:NS].rearrange("p (a b) -> p a b", a=SB),
                    tp[:, 0:NS].rearrange("p (a b) -> p a b", a=SB), ALU.mult), [("ps", ba), tt], [("ext", cc)])
                ds_ = DG.get(dgw[cc], ('dg', cc))
                by = bank()
                for jt in range(CW):
                    MM(PS[by][:, 0:NS], dgbuf[ds_][:, jt, :], ext[:, cc, :, jt:jt + DS], jt == 0, jt == CW - 1,
                       [("ext", cc), ("dg", ds_)], [("ps", by)])
                ACT(yvs[:, cc, :], PS[by][:, 0:NS], AF.Identity, [("ps", by)], [("yvs", cc)], bias=col(C_CB + cc))
                tq, tqt = newtmp()
                ACT(tq[:, 0:NS], yvs[:, cc, :], AF.Square, [("yvs", cc)], [tqt])
                MM(PS[BSUM][:, 0:NS], ones_f, yvs[:, cc, :], cc == 0, cc == 7, [("yvs", cc)], [("ps", BSUM)])
                MM(PS[BSQ][:, 0:NS], ones_f, tq[:, 0:NS], cc == 0, cc == 7, [tqt], [("ps", BSQ)])
        mean, mk_ = newtmp()
        msq, qk_ = newtmp()
        rstd, rk_ = newtmp()
        nmr, nk_ = newtmp()
        mean, msq, rstd, nmr = mean[:, 0:NS], msq[:, 0:NS], rstd[:, 0:NS], nmr[:, 0:NS]
        ACT(mean, PS[BSUM][:, 0:NS], AF.Copy, [("ps", BSUM)], [mk_], scale=1.0 / D)
        ACT(msq, PS[BSUM][:, 0:NS], AF.Square, [("ps", BSUM)], [qk_], scale=1.0 / D)
        STT(msq, PS[BSQ][:, 0:NS], 1.0 / D, msq, ALU.mult, ALU.subtract, [("ps", BSQ), qk_], [qk_])
        TS("dve", msq, msq, EPS, None, ALU.add, None, [qk_], [qk_])
        ACT(rstd, msq, AF.Ln, [qk_], [rk_])
        ACT(rstd, rstd, AF.Exp, [rk_], [rk_], scale=-0.5)
        STT(nmr, mean, -1.0, rstd, ALU.mult, ALU.mult, [mk_, rk_], [nk_])
        for cc in range(8):
            TT("dve", yvs[:, cc, :], yvs[:, cc, :], rstd, ALU.mult, [("yvs", cc), rk_], [("yvs", cc)])
            TT("pool", yvs[:, cc, :], yvs[:, cc, :], nmr, ALU.add, [("yvs", cc), nk_], [("yvs", cc)])
            ACT(zTs[:, cc, :], yvs[:, cc, :], AF.Silu, [("yvs", cc)], [("big", 24 + cc)],
                scale=col(C_LNG + cc), bias=col(C_LNB + cc))
        for j in range(2):
            sq_ = W.get(*wblock(w_in, 0, 8192 + j * 512))
            sc_ = W.get(*wblock(w_co, 0, j * 512))
            b1, b2 = bank(), bank()
            for kc in range(8):
                MM(PS[b1][0:NP], hTs[:, kc, :], wbuf[sq_][:, kc, :], kc == 0, kc == 7, HT + [("w", sq_)], [("ps", b1)])
            for kc in range(8):
                MM(PS[b2][0:NP], zTs[:, kc, :], wbuf[sc_][:, kc, :], kc == 0, kc == 7,
                   [("big", 24 + kc), ("w", sc_)], [("ps", b2)])
            tp, tt = newtmp()
            ACT(tp[0:NP], PS[b1][0:NP], AF.Sigmoid, [("ps", b1)], [tt])
            tq, tqt = newtmp()
            TT("dve", tq[0:NP], PS[b2][0:NP], tp[0:NP], ALU.mult, [("ps", b2), tt], [tqt])
            mg = mgs[0:NP, j * 512:(j + 1) * 512]
            mt = [MG(2 * j), MG(2 * j + 1)]
            TT("pool", mg, mg, tq[0:NP], ALU.add, [tqt] + mt, mt)

        cur_pool[0] = (0, 1, 2, 3)
        STAGE[0] = 's_tail'
        tp, tt = newtmp()
        mbv = tp.bitcast(BF16)
        ACT(mbv[0:NP], mgs[0:NP], AF.Copy, [MG(q) for q in range(4)], [tt])
        b = bank()
        for dc in range(8):
            TR(PSB[b][:, dc * 128:dc * 128 + NP], mbv[0:NP, dc * 128:(dc + 1) * 128], ident_b[0:NP, 0:NP], [tt], [("ps", b)])
        S.op("dve", lambda e, b=b: e.tensor_copy(actT[:, :, 0:NS], PSB[b].rearrange("p (a b) -> p a b", a=8)[:, :, 0:NP]),
             [("ps", b)], [("actT", 0)])
        for j in range(2):
            def ocons(b, j=j):
                xv = xts[0:NP, j * 512:(j + 1) * 512]
                TT("dve", xv, PS[b][0:NP], xv, ALU.add, [("ps", b), ("xt", 0)], [("xt", 0)])
            tm_block(w_out, 0, j * 512, ocons)
        rmsnorm_T(lambda c: xts[0:NP], NP, 1, C_GFFN, actT, lambda c: ("xt", 0), lambda c: ("actT", 0))
        for j in range(8):
            def ucons(m, b, j=j):
                f = 4 * j + m
                tp, tt = newtmp()
                ACT(tp[:, 0:NS], PS[b][:, 0:NS], AF.Square, [("ps", b)], [tt])
                STT(rTs[:, f, :], PS[b][:, 0:NS], 0.0, tp[:, 0:NS], ALU.is_gt, ALU.mult, [("ps", b), tt], [("big", f)])
            fm_block(w_up, 0, j * 512, ucons)
        for ch in range(2):
            bkk = 4 + ch
            for fg in range(4):
                slot = W.get(*wblock(w_dn, fg * 1024, ch * 512))
                for fc in range(8):
                    MM(PS[bkk][0:NP], rTs[:, fg * 8 + fc, :], wbuf[slot][:, fc, :], fg == 0 and fc == 0,
                       fg == 3 and fc == 7, [("big", fg * 8 + fc), ("w", slot)], [("ps", bkk)])
            xv = xts[0:NP, ch * 512:(ch + 1) * 512]
            TT("dve", xv, PS[bkk][0:NP], xv, ALU.add, [("ps", bkk), ("xt", 0)], [("xt", 0)])
        x = xts[0:NP]
        yo = xsb[0][0:NP]
        ssq, rs, rstd_, lt = smallf[0:NP, 0:1], smallf[0:NP, 1:2], smallf[0:NP, 2:3], smallf[0:NP, 3:4]
        ACT(yo, x, AF.Square, [("xt", 0)], [("xs", 0), "s_ssq"], accum=ssq)
        TS("dve", rs, ssq, 1.0 / D, EPS, ALU.mult, ALU.add, ["s_ssq"], ["s_rs"])
        ACT(lt, rs, AF.Ln, ["s_rs"], ["s_lt"])
        ACT(rstd_, lt, AF.Exp, ["s_lt"], ["s_rstd"], scale=-0.5)
        STT(yo, x, rstd_, rowbt[0:NP, D:2 * D], ALU.mult, ALU.mult, [("xt", 0), "s_rstd"], [("xs", 0)])
        S.dma("pool", lambda e: e.dma_start(out=ysm, in_=yo), r=[("xs", 0)], w=["o_ysm"])

    S.op("dve", lambda e: e.memset(S32, 0.0), w=[("S32", h) for h in range(NH)])
    S.op("dve", lambda e: e.memset(Sbf, 0.0), w=[("Sbf", h) for h in range(NH)])
    memkv()
    for ti in range(ntiles):
        prompt_tile(ti)
    if int(os.environ.get('K_SAMPLE', 1)):
        sample_pass()
    print("arena used words:", A.hi, "of", NW, " ops:", len(S.ops))
    S.analyze()
    LAST_SCHED[0] = S
    S.emit(nc, es)
    es.close()
    return nc


def host_consts(inp):
    f32 = np.float32
    lg = log_gammas()
    c = {}
    cvec = np.zeros((128, NCV), f32)

    def colfill(c0, vec):
        cvec[:, c0:c0 + 8] = np.asarray(vec, f32).reshape(8, 128).T
    colfill(0, inp["g_mix"][0])
    colfill(8, inp["g_ffn"][0])
    colfill(16, inp["g_mem"][0])
    colfill(24, inp["conv_b"][0])
    colfill(32, inp["conv_ln_g"][0])
    colfill(40, inp["conv_ln_b"][0])
    p = np.arange(128, dtype=np.float64)
    for h in range(NH):
        cvec[:, 48 + h] = np.exp(lg[h] * (p + 1))
        cvec[:, 52 + h] = np.exp(2 * lg[h] * (p + 1))
        cvec[:, 56 + h] = np.exp(lg[h] * (127 - p)) / 16.0
        t = (np.arange(128) % DS).astype(np.float64)
        cvec[:, 60 + h] = np.exp(lg[h] * (t + 1))
        cvec[:, 64 + h] = np.exp(2 * lg[h] * (t + 1))
        cvec[:, 68 + h] = np.exp(lg[h] * (DS - 1 - t)) / 16.0
    for s in range(SB):
        cvec[:, 72 + s] = ((np.arange(128) // DS) == s).astype(f32)
    c["cvec"] = cvec
    c["rowb"] = np.ascontiguousarray(np.broadcast_to(
        np.concatenate([inp["ret_gn_g"][0], inp["g_final"]]).astype(f32)[None, :], (128, 2 * D)))
    inv = (f32(10000.0) ** (-(np.arange(0, HD, 2).astype(f32)) / f32(HD))).astype(f32)
    pos = np.arange(T).astype(f32)
    ang = (pos[None, :] * inv[:, None]).astype(f32)
    c["cosp"] = np.cos(ang.astype(np.float64)).astype(f32)
    c["sinp"] = np.sin(ang.astype(np.float64)).astype(f32)
    poss = (PAST + (np.arange(NS) % DS)).astype(f32)
    angs = (poss[None, :] * inv[:, None]).astype(f32)
    c["css"] = np.ascontiguousarray(np.stack([np.cos(angs.astype(np.float64)), np.sin(angs.astype(np.float64))], 1).astype(f32))
    jj = np.arange(128)[:, None]
    ii = np.arange(128)[None, :]
    maskp = np.zeros((128, NH, 128), f32)
    masks = np.zeros((128, NH, 64), f32)
    js = np.arange(64)[:, None]
    is_ = np.arange(64)[None, :]
    for h in range(NH):
        maskp[:, h, :] = np.where(ii >= jj, np.exp(-lg[h] * (jj + 1)) / 16.0, 0.0)
        same = (js // DS) == (is_ // DS)
        masks[:64, h, :] = np.where(same & ((is_ % DS) >= (js % DS)), np.exp(-lg[h] * ((js % DS) + 1)) / 16.0, 0.0)
    c["maskp"] = maskp
    c["masks"] = masks
    c["identf"] = np.eye(128, dtype=f32)
    selm = np.zeros((128, SB, 64), f32)
    for s in range(SB):
        selm[:, s, s * DS:(s + 1) * DS] = 1.0
    c["selm"] = selm
    cw = np.asarray(inp["conv_w"][0], f32)
    dgw = np.zeros((8, 128, CW, 128), f32)
    ar = np.arange(128)
    for cc in range(8):
        dgw[cc, ar, :, ar] = cw[:, cc * 128:(cc + 1) * 128].T
    c["dgw"] = dgw
    return c


_NC = None


def kernel(x_prompt, x_sample, mem_prompt, state_ret, state_conv, cache_mem_k, cache_mem_v,
           g_mix, w_in, ret_gn_g, conv_w, conv_b, conv_ln_g, conv_ln_b, w_conv_out, w_out,
           g_ffn, w_up, w_down, g_mem, w_mem_kv, g_final):
    global _NC
    f32 = np.float32
    inp = dict(g_mix=g_mix, ret_gn_g=ret_gn_g, conv_w=conv_w, conv_b=conv_b, conv_ln_g=conv_ln_g,
               conv_ln_b=conv_ln_b, g_ffn=g_ffn, g_mem=g_mem, g_final=g_final)
    inp = {k: np.asarray(v, f32) for k, v in inp.items()}
    hc = host_consts(inp)
    if _NC is None:
        _NC = build()
    nc = _NC
    shared = dict(
        w_in=np.ascontiguousarray(np.asarray(w_in, f32)[0]),
        w_co=np.ascontiguousarray(np.asarray(w_conv_out, f32)[0]),
        w_out=np.ascontiguousarray(np.asarray(w_out, f32)[0]),
        w_up=np.ascontiguousarray(np.asarray(w_up, f32)[0]),
        w_dn=np.ascontiguousarray(np.asarray(w_down, f32)[0]),
        w_kv=np.ascontiguousarray(np.asarray(w_mem_kv, f32)[0]),
        **hc)
    x_prompt = np.asarray(x_prompt, f32)
    x_sample = np.asarray(x_sample, f32)
    mem_prompt = np.asarray(mem_prompt, f32)
    state_ret = np.asarray(state_ret, f32)
    state_conv = np.asarray(state_conv, f32)
    cache_mem_k = np.asarray(cache_mem_k, f32)
    cache_mem_v = np.asarray(cache_mem_v, f32)
    in_maps = []
    for i in range(NCORES):
        sl = slice(i * SB, (i + 1) * SB)
        m = dict(shared)
        m["xp"] = np.ascontiguousarray(x_prompt[i])
        m["xsm"] = np.ascontiguousarray(x_sample[sl].reshape(NS, D))
        m["mem"] = np.ascontiguousarray(mem_prompt[i])
        m["sret"] = np.ascontiguousarray(state_ret[0, sl])
        m["sconv"] = np.ascontiguousarray(state_conv[0, sl])
        m["ck"] = np.ascontiguousarray(cache_mem_k[0, sl].reshape(SB, NMEM, D))
        m["cv"] = np.ascontiguousarray(cache_mem_v[0, sl].reshape(SB, NMEM, D))
        in_maps.append(m)
    res = run_bass_kernel_spmd(nc, in_maps, core_ids=list(range(NCORES)))
    R = res.results
    for k in R[0]:
        if k.startswith('dbg_'):
            DEBUG_OUT[k] = np.asarray(R[0][k])
    y_prompt = np.stack([R[i]["yp"] for i in range(NCORES)], 0)
    y_sample = np.concatenate([R[i]["ysm"].reshape(SB, DS, D) for i in range(NCORES)], 0)
    srp_o = np.stack([R[i]["srp"] for i in range(NCORES)], 0)[None]
    scp_o = np.stack([R[i]["scp"] for i in range(NCORES)], 0)[None]
    mk_o = np.stack([R[i]["mkp"].reshape(NMEM, NH, HD) for i in range(NCORES)], 0)[None]
    mv_o = np.stack([R[i]["mvp"].reshape(NMEM, NH, HD) for i in range(NCORES)], 0)[None]
    srs_o = np.concatenate([R[i]["srs"] for i in range(NCORES)], 0)[None]
    scs_o = np.concatenate([R[i]["scs"] for i in range(NCORES)], 0)[None]
    return (y_prompt.astype(f32), y_sample.astype(f32), srp_o.astype(f32), scp_o.astype(f32),
            mk_o.astype(f32), mv_o.astype(f32), srs_o.astype(f32), scs_o.astype(f32))
```

```python
import numpy as np
from contextlib import ExitStack
import concourse.bass as bass
import concourse.mybir as mybir
from concourse.bass_utils import run_bass_kernel_spmd

dt = mybir.dt
F32 = dt.float32
BF16 = dt.bfloat16
AF = mybir.ActivationFunctionType
ALU = mybir.AluOpType

NCORES = 8
D = 1024
DIN = 10240
DFF = 4096
T = 2048
NT = 512
NTILES = T // NT
NH = 4
HD = 256
NMEM = 256
CW = 31
PAST = 16384
SB = 16
DS = 4
NS = SB * DS
EPS = 1e-6
NCV = 96

DEBUG_OUT = {}
STAGE = ['init']
LAST_SCHED = [None]
DEBUG = False


class Op:
    __slots__ = ("eng", "fn", "r", "w", "dma", "eidx", "signal", "waits", "sem", "val", "ringwait", "marker", "barrier", "stage")


class Sched:
    ENGS = ("pe", "act", "dve", "pool", "sp")

    def __init__(self):
        self.ops = []

    def _mk(self, eng, fn, r, w, dma):
        o = Op()
        o.eng, o.fn, o.r, o.w, o.dma = eng, fn, tuple(r), tuple(w), dma
        o.signal = False
        o.waits = []
        o.marker = False
        o.barrier = False
        o.stage = STAGE[0]
        o.ringwait = None
        return o

    def op(self, eng, fn, r=(), w=()):
        o = self._mk(eng, fn, r, w, False)
        self.ops.append(o)
        return o

    def dma(self, q, fn, r=(), w=()):
        o = self._mk(q, fn, r, w, True)
        self.ops.append(o)
        return o

    def make_dma(self, q, fn, r=(), w=()):
        return self._mk(q, fn, r, w, True)

    def marker(self):
        o = self._mk(None, None, (), (), False)
        o.marker = True
        self.ops.append(o)
        return o

    def barrier(self):
        for e in ("pe", "act", "dve", "pool", "sp"):
            o = self._mk(e, (lambda en: en.nop()), (), (), False)
            o.barrier = True
            self.ops.append(o)

    def insert_after(self, marker, op):
        i = self.ops.index(marker)
        self.ops.insert(i + 1, op)

    def analyze(self):
        self.ops = [o for o in self.ops if not o.marker]
        import os
        lim = int(os.environ.get('K_LIMIT', 10**9))
        self.ops = self.ops[:lim]
        eng_ops = {e: [] for e in self.ENGS}
        for o in self.ops:
            o.eidx = len(eng_ops[o.eng])
            eng_ops[o.eng].append(o)
        self.eng_ops = eng_ops
        last_w = {}
        readers = {}
        waited = {e: {} for e in self.ENGS}
        for o in self.ops:
            deps = []
            for t in o.r:
                d = last_w.get(t)
                if d is not None:
                    deps.append(d)
                if isinstance(t, tuple) and t[0] == "ps":
                    deps.extend(x for x in readers.get(t, ()) if x.eng != o.eng)
            for t in o.w:
                d = last_w.get(t)
                if d is not None:
                    deps.append(d)
                deps.extend(readers.get(t, ()))
            if o.barrier:
                pos = self.ops.index(o)
                seen_dma = {}
                for x in self.ops[:pos]:
                    if x.barrier:
                        continue
                    if x.dma:
                        seen_dma.setdefault(x.eng, []).append(x)
                    elif x.eng != o.eng:
                        deps.append(x) if False else None
                lastc = {}
                for x in self.ops[:pos]:
                    if not x.dma and not x.barrier and x.eng != o.eng:
                        lastc[x.eng] = x
                deps.extend(lastc.values())
                for q, lst in seen_dma.items():
                    deps.extend(lst[-8:])
            need = {}
            for d in deps:
                if d is o:
                    continue
                if d.dma:
                    need[("dma", id(d))] = d
                else:
                    if d.eng == o.eng and not o.dma:
                        if o.eng == "pe":
                            continue
                        if o.eng != "pool" and o.eidx - d.eidx > 3:
                            continue
                    k = d.eng
                    if k not in need or need[k].eidx < d.eidx:
                        need[k] = d
            wd = waited[o.eng]
            for k, d in need.items():
                if d.dma:
                    if k in wd:
                        continue
                    wd[k] = True
                else:
                    if wd.get(k, -1) >= d.eidx:
                        continue
                    wd[k] = d.eidx
                d.signal = True
                o.waits.append(d)
            for t in o.r:
                readers.setdefault(t, []).append(o)
            for t in o.w:
                last_w[t] = o
                readers[t] = []

    def emit(self, nc, es):
        RING = 8
        sems = {e: es.enter_context(nc.semaphore("sem_" + e)) for e in ("pe", "act", "dve", "pool")}
        rings = {q: [es.enter_context(nc.semaphore("ring_%s_%d" % (q, i))) for i in range(RING)]
                 for q in ("sp", "pool", "act")}
        for e in ("pe", "act", "dve", "pool"):
            cnt = 0
            for o in self.eng_ops[e]:
                if o.dma:
                    continue
                if o.signal:
                    cnt += 1
                    o.sem, o.val = sems[e], cnt
        all_dma = []
        for q in ("sp", "pool", "act"):
            n = 0
            for o in self.eng_ops[q]:
                if not o.dma:
                    continue
                o.sem = rings[q][n % RING]
                o.val = 16 * (n // RING + 1)
                if n >= RING:
                    o.ringwait = (o.sem, 16 * (n // RING))
                n += 1
                all_dma.append(o)
        final = {}
        for o in all_dma:
            final[id(o.sem)] = (o.sem, max(final.get(id(o.sem), (None, 0))[1], o.val))

        def run(name, e):
            for o in self.eng_ops[name]:
                if o.ringwait is not None:
                    e.wait_ge(o.ringwait[0], o.ringwait[1])
                for d in o.waits:
                    e.wait_ge(d.sem, d.val)
                ins = o.fn(e)
                if o.dma:
                    ins.then_inc(o.sem, 16)
                elif o.signal:
                    ins.then_inc(o.sem, 1)
            if name == "sp":
                for sem, val in final.values():
                    e.wait_ge(sem, val)

        with nc.Block() as block:
            @block.tensor
            def _(e):
                run("pe", e)

            @block.scalar
            def _(e):
                run("act", e)

            @block.vector
            def _(e):
                run("dve", e)

            @block.gpsimd
            def _(e):
                run("pool", e)

            @block.sync
            def _(e):
                run("sp", e)


class Arena:
    def __init__(self, ap, nwords):
        self.ap = ap
        self.n = nwords
        self.off = 0
        self.hi = 0

    def tile(self, free_shape, dtype, parts=128):
        nel = int(np.prod(free_shape))
        nw = nel if dtype == F32 else (nel + 1) // 2
        nw = (nw + 7) // 8 * 8
        assert self.off + nw <= self.n, ("arena overflow", self.off, nw, self.n)
        v = self.ap[:, self.off:self.off + nw]
        self.off += nw
        self.hi = max(self.hi, self.off)
        if dtype != F32:
            v = v.bitcast(dtype)
        v = v[:, 0:nel]
        if len(free_shape) == 2:
            v = v.rearrange("p (a b) -> p a b", a=free_shape[0])
        elif len(free_shape) == 3:
            v = v.rearrange("p (a b c) -> p a b c", a=free_shape[0], b=free_shape[1])
        if parts != 128:
            v = v[0:parts]
        return v


def log_gammas():
    return np.log1p(-np.exp2(-5.0 - np.arange(NH, dtype=np.float64)))


def build(ntiles=None):
    import os
    if ntiles is None:
        ntiles = int(os.environ.get('K_NTILES', NTILES))
    nc = bass.Bass("TRN2", target_bir_lowering=False)
    es = ExitStack()

    def din(name, shape):
        return nc.dram_tensor(name, list(shape), F32, kind="ExternalInput").ap()

    def dout(name, shape):
        return nc.dram_tensor(name, list(shape), F32, kind="ExternalOutput").ap()

    xp = din("xp", [T, D])
    xsm = din("xsm", [NS, D])
    mem = din("mem", [NMEM, D])
    sret = din("sret", [SB, NH, HD, HD])
    sconv = din("sconv", [SB, 30, D])
    ck = din("ck", [SB, NMEM, D])
    cv = din("cv", [SB, NMEM, D])
    w_in = din("w_in", [D, DIN])
    w_co = din("w_co", [D, D])
    w_out = din("w_out", [D, D])
    w_up = din("w_up", [D, DFF])
    w_dn = din("w_dn", [DFF, D])
    w_kv = din("w_kv", [D, 2 * D])
    dgw = din("dgw", [8, 128, CW, 128])
    cvec = din("cvec", [128, NCV])
    rowb = din("rowb", [128, 2 * D])
    cosp = din("cosp", [128, T])
    sinp = din("sinp", [128, T])
    css = din("css", [128, 2, NS])
    maskp = din("maskp", [128, NH, 128])
    masks = din("masks", [128, NH, 64])
    identf = din("identf", [128, 128])
    selm = din("selm", [128, SB, 64])

    yp = dout("yp", [T, D])
    ysm = dout("ysm", [NS, D])
    srp = dout("srp", [NH, HD, HD])
    scp = dout("scp", [30, D])
    mkp = dout("mkp", [NMEM, D])
    mvp = dout("mvp", [NMEM, D])
    srs = dout("srs", [SB, NH, HD, HD])
    scs = dout("scs", [SB, 30, D])

    S = Sched()
    dbg_n = [0]

    def dbg_dump(name, ap4, r):
        if not DEBUG:
            return
        d = dout("dbg_" + name, [NT, D])
        S.dma("sp", lambda e: e.dma_start(out=d.rearrange("(c p) f -> p c f", p=128), in_=ap4), r=r, w=[("dbg", name)])
    LG = log_gammas()
    GC = [float(np.exp(LG[h] * 128)) for h in range(NH)]
    G4 = [float(np.exp(LG[h] * DS)) for h in range(NH)]

    NW = 53000
    arena_t = es.enter_context(nc.sbuf_tensor("arena", [128, NW], F32))
    A = Arena(arena_t[:], NW)
    PS = [es.enter_context(nc.psum_tensor("ps%d" % i, [128, 512], F32))[:] for i in range(8)]
    PSB = [p.bitcast(BF16) for p in PS]

    ident_f = A.tile([128], F32)
    ident_b = A.tile([128], BF16)
    ones_f = A.tile([128], F32)
    cvt = A.tile([NCV], F32)
    rowbt = A.tile([2 * D], F32)
    maskt = A.tile([NH, 128], F32)
    maskst = A.tile([NH, 64], F32)
    cst_s = A.tile([2, NS], F32)
    selt = A.tile([SB, 64], BF16)
    C_GMIX, C_GFFN, C_GMEM, C_CB, C_LNG, C_LNB = 0, 8, 16, 24, 32, 40
    C_QD, C_QD2, C_KDEC = 48, 52, 56
    C_SQD, C_SQD2, C_SKDEC = 60, 64, 68
    C_SEL = 72

    S.dma("sp", lambda e: e.dma_start(out=ident_f, in_=identf), w=["c_identf"])
    S.dma("pool", lambda e: e.dma_start(out=ident_b, in_=identf), w=["c_identb"])
    S.dma("pool", lambda e: e.dma_start(out=selt, in_=selm), w=["c_sel"])
    S.dma("sp", lambda e: e.dma_start(out=cvt, in_=cvec), w=["c_cv"])
    S.dma("sp", lambda e: e.dma_start(out=rowbt, in_=rowb), w=["c_rowb"])
    S.dma("sp", lambda e: e.dma_start(out=maskt, in_=maskp), w=["c_mask"])
    S.dma("sp", lambda e: e.dma_start(out=maskst, in_=masks), w=["c_masks"])
    S.dma("sp", lambda e: e.dma_start(out=cst_s, in_=css), w=["c_css"])
    S.op("dve", lambda e: e.memset(ones_f, 1.0), w=["c_ones"])
    CONST = ["c_identf", "c_identb", "c_cv", "c_rowb", "c_mask", "c_masks", "c_css", "c_ones", "c_sel"]

    NSLOT = 4
    wbuf = [A.tile([8, 512], BF16) for _ in range(NSLOT)]
    dgbuf = [A.tile([CW, 128], BF16) for _ in range(2)]
    NTMP = 6
    tmp = [A.tile([512], F32) for _ in range(NTMP)]
    xsb = [A.tile([D], F32) for _ in range(2)]
    actT = A.tile([8, NT], BF16)
    smallf = A.tile([64], F32)
    stt = A.tile([NH, 6], F32)
    mvt = A.tile([NH, 2], F32)
    sTb = [A.tile([128], BF16) for _ in range(4)]
    eT = [A.tile([2, NT], BF16) for _ in range(2)]
    mark_common = A.off

    xt = A.tile([4, D], F32)
    merged = A.tile([4, D], F32)
    big = A.tile([32, NT], BF16)
    uT = A.tile([8, 30 + NT], BF16)
    v_bf = A.tile([4, D], BF16)
    S32 = A.tile([8, HD], F32)
    Sbf = A.tile([8, HD], BF16)
    kTm = A.tile([8, NMEM], BF16)
    vaug = A.tile([2, NH, HD + 1], BF16)
    cs = A.tile([2, NT], F32)

    qT = big[:, 0:8, :]
    kT = big[:, 8:16, :]
    qxT = big[:, 16:24, :]
    khat = big[:, 24:32, :].rearrange("p a b -> p (a b)").rearrange("p (c f) -> p c f", c=4)
    yv = xt.rearrange("p a b -> p (a b)").rearrange("p (a b) -> p a b", a=8)
    rT = big

    tmp_i = [0]

    tmp_pool = [tuple(range(NTMP))]

    def newtmp():
        pool = tmp_pool[0]
        i = pool[tmp_i[0] % len(pool)]
        tmp_i[0] += 1
        return tmp[i], ("tmp", i)

    bank_i = [0]

    cur_pool = [(0, 1, 2, 3, 4, 5, 6, 7)]

    def bank():
        pool = cur_pool[0]
        b = pool[bank_i[0] % len(pool)]
        bank_i[0] += 1
        return b

    def ACT(out, in_, func, r, w, scale=None, bias=None, accum=None):
        kw = {}
        if scale is not None:
            kw["scale"] = scale
        if bias is not None:
            kw["bias"] = bias
        if accum is not None:
            kw["accum_out"] = accum
        S.op("act", lambda e: e.activation(out, in_, func, **kw), r, w)

    def TT(eng, out, a, b, op, r, w):
        S.op(eng, lambda e: e.tensor_tensor(out, a, b, op), r, w)

    def STT(out, in0, scalar, in1, op0, op1, r, w):
        S.op("dve", lambda e: e.scalar_tensor_tensor(out, in0, scalar, in1, op0, op1), r, w)

    def TS(eng, out, in0, s1, s2, op0, op1, r, w):
        if op1 is None:
            S.op(eng, lambda e: e.tensor_scalar(out, in0, s1, None, op0), r, w)
        else:
            S.op(eng, lambda e: e.tensor_scalar(out, in0, s1, s2, op0, op1), r, w)

    def MM(out, lhsT, rhs, start, stop, r, w):
        S.op("pe", lambda e: e.matmul(out, lhsT, rhs, start=start, stop=stop), r, w)

    def TR(out, in_, ident, r, w):
        S.op("pe", lambda e: e.transpose(out, in_, ident), r, w)

    class WStream:
        def __init__(self, views, tok, look, shape):
            self.views, self.tok, self.look, self.shape = views, tok, look, shape
            self.n = len(views)
            self.i = 0
            self.anchors = []
            self.scr = {}

        def get(self, src, key=None):
            i = self.i
            self.i += 1
            slot = i % self.n
            dst = self.views[slot]
            store = None
            if key is not None and key in self.scr:
                sap = self.scr[key]
                op = S.make_dma("sp", lambda e: e.dma_start(out=dst, in_=sap), r=[("scr", self.tok, key)],
                                w=[(self.tok, slot)])
            else:
                op = S.make_dma("pool", lambda e: e.dma_start(out=dst, in_=src, max_dma_last_dim=2048),
                                w=[(self.tok, slot)])
                if key is not None:
                    sap = nc.dram_tensor("scr_%s_%d" % (self.tok, len(self.scr)), [128] + list(self.shape), BF16,
                                         kind="Internal").ap()
                    self.scr[key] = sap
                    store = S.make_dma("sp", lambda e: e.dma_start(out=sap, in_=dst), r=[(self.tok, slot)],
                                       w=[("scr", self.tok, key)])
            if i >= self.look:
                S.insert_after(self.anchors[i - self.look], op)
            else:
                S.ops.append(op)
            self.anchors.append(S.marker())
            if store is not None:
                S.ops.append(store)
            return slot

    W = WStream(wbuf, "w", 2, [8, 512])
    DG = WStream(dgbuf, "dg", 1, [CW, 128])

    wnames = {}

    def wblock(wap, r0, c0):
        return (wap[r0:r0 + 1024, c0:c0 + 512].rearrange("(kc p) n -> p kc n", p=128), (wap.tensor.name, r0, c0))

    col = lambda c0, n=1: cvt[:, c0:c0 + n]

    def rsqrt_cols(src, dst, n, r, w):
        lt = smallf[:, 48:48 + n]
        ACT(lt, src, AF.Ln, r, ["s_ln"])
        ACT(dst, lt, AF.Exp, ["s_ln"], w, scale=-0.5)

    def rms_A(x, xtokc, c, nparts):
        xs = xsb[c % 2][0:nparts]
        xst = ("xs", c % 2)
        ssq = smallf[0:nparts, 0:1]
        rs = smallf[0:nparts, 1:2]
        rstd = smallf[0:nparts, 2:3]
        ACT(xs, x, AF.Square, [xtokc], [xst, "s_ssq"], accum=ssq)
        TS("dve", rs, ssq, 1.0 / D, EPS, ALU.mult, ALU.add, ["s_ssq"], ["s_rs"])
        lt = smallf[0:nparts, 3:4]
        ACT(lt, rs, AF.Ln, ["s_rs"], ["s_lt"])
        ACT(rstd, lt, AF.Exp, ["s_lt"], ["s_rstd"], scale=-0.5)
        ACT(xs, x, AF.Copy, [xtokc, "s_rstd"], [xst], scale=rstd)

    def rms_B(c, nparts, gcol, dstT, dtokc):
        xs = xsb[c % 2][0:nparts]
        xst = ("xs", c % 2)
        for half in range(2):
            b = bank()
            for q in range(4):
                dc = half * 4 + q
                TR(PS[b][:, q * 128:q * 128 + nparts], xs[:, dc * 128:(dc + 1) * 128], ident_f[0:nparts, 0:nparts],
                   [xst], [("ps", b)])
            src3 = PS[b].rearrange("p (a b) -> p a b", a=4)[:, :, 0:nparts]
            g3 = cvt[:, gcol + half * 4:gcol + half * 4 + 4].unsqueeze(2).broadcast_to([128, 4, nparts])
            TT("dve", dstT[:, half * 4:half * 4 + 4, c * 128:c * 128 + nparts], src3, g3, ALU.mult,
               [("ps", b)], [dtokc])

    def rmsnorm_T(src, nparts, ntok_chunks, gcol, dstT, xtok, dtok):
        for c in range(ntok_chunks):
            rms_A(src(c), xtok(c), c, nparts)
            rms_B(c, nparts, gcol, dstT, dtok(c))

    xin = A.tile([D], F32)

    def prep_A(ti, c):
        t0 = ti * NT
        if c == 0:
            S.dma("sp", lambda e: e.dma_start(out=cs[:, 0, :], in_=cosp[:, t0:t0 + NT]), w=["cs"])
            S.dma("sp", lambda e: e.dma_start(out=cs[:, 1, :], in_=sinp[:, t0:t0 + NT]), w=["cs"])
        S.dma("act", lambda e: e.dma_start(out=xin, in_=xp[t0 + c * 128:t0 + (c + 1) * 128, :]), w=["xin"])
        rms_A(xin, "xin", c, 128)

    def prep_B(c):
        rms_B(c, 128, C_GMIX, actT, ("actT", c))

    def prep_hT(ti):
        for c in range(4):
            prep_A(ti, c)
            prep_B(c)

    for eng in ("pe", "act", "dve", "pool"):
        if eng == "pe":
            S.op("pe", lambda e: e.transpose(PS[7][:, 0:128], ident_f, ident_f), CONST, [("ps", 7)])
        elif eng == "act":
            S.op("act", lambda e: e.activation(smallf[:, 60:61], cvt[:, 0:1], AF.Copy), CONST, ["s_junk_a"])
        elif eng == "dve":
            S.op("dve", lambda e: e.tensor_copy(smallf[:, 61:62], cvt[:, 0:1]), CONST, ["s_junk_d"])
        else:
            S.op("pool", lambda e: e.tensor_copy(smallf[:, 62:63], cvt[:, 0:1]), CONST, ["s_junk_p"])

    def memkv():
        for c2 in range(2):
            S.dma("sp", lambda e, c2=c2: e.dma_start(out=xt[:, c2, :], in_=mem[c2 * 128:(c2 + 1) * 128, :]),
                  w=[("xt", c2)])
        rmsnorm_T(lambda c: xt[:, c, :], 128, 2, C_GMEM, actT, lambda c: ("xt", c), lambda c: ("actT", c))
        S.op("dve", lambda e: e.memset(vaug[:, :, :, HD:HD + 1], 1.0), w=["vaug"])
        for j in range(4):
            slot = W.get(wblock(w_kv, 0, j * 512)[0])
            wt = ("w", slot)
            for c2 in range(2):
                b = bank()
                for kc in range(8):
                    MM(PS[b], actT[:, kc, c2 * 128:(c2 + 1) * 128], wbuf[slot][:, kc, :], kc == 0, kc == 7,
                       [("actT", c2), wt], [("ps", b)])
                tp, tt = newtmp()
                ACT(tp, PS[b], AF.Copy, [("ps", b)], [tt])
                dst = (mkp if j < 2 else mvp)[c2 * 128:(c2 + 1) * 128, (j % 2) * 512:(j % 2) * 512 + 512]
                S.dma("pool", lambda e, dst=dst, tp=tp: e.dma_start(out=dst, in_=tp), r=[tt], w=[("o_kv", j, c2)])
                if j >= 2:
                    jj = j - 2
                    S.op("dve", lambda e, tp=tp, c2=c2, jj=jj: e.tensor_copy(
                        vaug[:, c2, 2 * jj:2 * jj + 2, 0:HD], tp.rearrange("p (a b) -> p a b", a=2)),
                        [tt], ["vaug"])
            if j < 2:
                for m in range(4):
                    b = bank()
                    for kc in range(8):
                        MM(PS[b][:, 0:NMEM], wbuf[slot][:, kc, m * 128:(m + 1) * 128], actT[:, kc, 0:NMEM],
                           kc == 0, kc == 7, [("actT", 0), ("actT", 1), wt], [("ps", b)])
                    S.op("dve", lambda e, b=b, j=j, m=m: e.tensor_copy(kTm[:, 4 * j + m, :], PS[b][:, 0:NMEM]),
                         [("ps", b)], ["kTm"])

    def prompt_tile(ti):
        t0 = ti * NT
        ACT_ALL = [("actT", c) for c in range(4)]
        STAGE[0] = 'S0'
        if ti == 0:
            prep_hT(0)

        def fm_block(wap, r0, c0, consumer, nm=4):
            slot = W.get(*wblock(wap, r0, c0))
            for m in range(nm):
                b = bank()
                for kc in range(8):
                    MM(PS[b], wbuf[slot][:, kc, m * 128:(m + 1) * 128], actT[:, kc, :], kc == 0, kc == 7,
                       ACT_ALL + [("w", slot)], [("ps", b)])
                consumer(m, b)

        def tm_block(wap, r0, c0, consumer):
            slot = W.get(*wblock(wap, r0, c0))
            for c in range(4):
                b = bank()
                for kc in range(8):
                    MM(PS[b], actT[:, kc, c * 128:(c + 1) * 128], wbuf[slot][:, kc, :], kc == 0, kc == 7,
                       [("actT", c), ("w", slot)], [("ps", b)])
                consumer(c, b)

        STAGE[0] = 'A_qk'
        def rope_consumer(dst, base_slab, j):
            st = {}

            def cons(m, b):
                st[m] = b
                if m % 2 == 1:
                    h = 2 * j + m // 2
                    b1, b2 = st[m - 1], st[m]
                    t1, k1 = newtmp()
                    t2, k2 = newtmp()
                    cos_, sin_ = cs[:, 0, :], cs[:, 1, :]
                    TT("dve", t1, PS[b1], cos_, ALU.mult, [("ps", b1), "cs"], [k1])
                    TT("dve", t2, PS[b2], sin_, ALU.mult, [("ps", b2), "cs"], [k2])
                    TT("pool", dst[:, 2 * h, :], t1, t2, ALU.subtract, [k1, k2], [("big", base_slab + 2 * h)])
                    t3, k3 = newtmp()
                    t4, k4 = newtmp()
                    TT("dve", t3, PS[b2], cos_, ALU.mult, [("ps", b2), "cs"], [k3])
                    TT("dve", t4, PS[b1], sin_, ALU.mult, [("ps", b1), "cs"], [k4])
                    TT("pool", dst[:, 2 * h + 1, :], t3, t4, ALU.add, [k3, k4], [("big", base_slab + 2 * h + 1)])
            return cons

        for j in range(2):
            fm_block(w_in, 0, 0 + j * 512, rope_consumer(qT, 0, j))
        for j in range(2):
            fm_block(w_in, 0, 1024 + j * 512, rope_consumer(kT, 8, j))
        STAGE[0] = 'A_vg'
        for j in range(2):
            def vcons(c, b, j=j):
                ACT(v_bf[:, c, j * 512:(j + 1) * 512], PS[b], AF.Copy, [("ps", b)], [("v", c)])
            tm_block(w_in, 0, 2048 + j * 512, vcons)
        for j in range(2):
            def gcons(c, b, j=j):
                tp, tt = newtmp()
                ACT(tp, PS[b], AF.Silu, [("ps", b)], [tt])
                TT("pool", merged[:, c, j * 512:(j + 1) * 512], tp, rowbt[:, j * 512:(j + 1) * 512], ALU.mult,
                   [tt], [("mg", c, 2 * j), ("mg", c, 2 * j + 1)])
            tm_block(w_in, 0, 3072 + j * 512, gcons)
        for j in range(2):
            def g0cons(c, b, j=j):
                tp, tt = newtmp()
                ACT(tp, PS[b], AF.Sigmoid, [("ps", b)], [tt])
                mg = merged[:, c, j * 512:(j + 1) * 512]
                TT("pool", mg, mg, tp, ALU.mult, [tt, ("mg", c, 2 * j), ("mg", c, 2 * j + 1)],
                   [("mg", c, 2 * j), ("mg", c, 2 * j + 1)])
            tm_block(w_in, 0, 7168 + j * 512, g0cons)

        STAGE[0] = 'A_ret'
        cur_pool[0] = (0, 1, 2, 3)
        def ret_stage1(c):
            cs_ = slice(c * 128, (c + 1) * 128)
            for h in range(NH):
                qk_r = [("big", 2 * h), ("big", 2 * h + 1), ("big", 8 + 2 * h), ("big", 8 + 2 * h + 1)]
                bk = bank()
                for e2 in range(2):
                    TR(PSB[bk][:, e2 * 128:(e2 + 1) * 128], kT[:, 2 * h + e2, cs_], ident_b,
                       [("big", 8 + 2 * h + e2)], [("ps", bk)])
                for e2 in range(2):
                    MM(PS[bk][:, 128:256], kT[:, 2 * h + e2, cs_], qT[:, 2 * h + e2, cs_], e2 == 0, e2 == 1,
                       qk_r, [("ps", bk)])
                ACT(khat[:, c, h * HD:(h + 1) * HD], PSB[bk][:, 0:HD], AF.Copy, [("ps", bk)], [("big", 24 + 2 * c + h // 2)],
                    scale=col(C_KDEC + h))
                sb_, sk = sTb[h], ("sTb", h)
                TT("dve", sb_, PS[bk][:, 128:256], maskt[:, h, :], ALU.mult, [("ps", bk)], [sk])

        def ret_stage2(c):
            cs_ = slice(c * 128, (c + 1) * 128)
            obanks = [4, 5]
            for h in range(NH):
                qk_r = [("big", 2 * h), ("big", 2 * h + 1), ("big", 8 + 2 * h), ("big", 8 + 2 * h + 1)]
                sb_, sk = sTb[h], ("sTb", h)
                bo = obanks[h // 2]
                oap = PS[bo][:, (h % 2) * HD:(h % 2 + 1) * HD]
                MM(oap, sb_, v_bf[:, c, h * HD:(h + 1) * HD], True, False, [sk, ("v", c)], [("ps", bo)])
                for e2 in range(2):
                    MM(oap, qT[:, 2 * h + e2, cs_], Sbf[:, 2 * h + e2, :], False, e2 == 1,
                       qk_r + [("Sbf", h)], [("ps", bo)])
                bp = bank()
                for e2 in range(2):
                    MM(PS[bp][:, e2 * HD:(e2 + 1) * HD], khat[:, c, h * HD + e2 * 128:h * HD + (e2 + 1) * 128],
                       v_bf[:, c, h * HD:(h + 1) * HD], True, True,
                       [("big", 24 + 2 * c + h // 2), ("v", c)], [("ps", bp)])
                s32v = S32[:, 2 * h:2 * h + 2, :].rearrange("p a b -> p (a b)")
                STT(s32v, s32v, GC[h], PS[bp], ALU.mult, ALU.add, [("ps", bp), ("S32", h)], [("S32", h)])
                ACT(Sbf[:, 2 * h:2 * h + 2, :].rearrange("p a b -> p (a b)"), s32v, AF.Copy, [("S32", h)], [("Sbf", h)])

        def ret_gn(c):
            obanks = [4, 5]
            for h in range(NH):
                bo = obanks[h // 2]
                S.op("dve", lambda e, h=h, bo=bo: e.bn_stats(stt[:, h, :], PS[bo][:, (h % 2) * HD:(h % 2 + 1) * HD]),
                     [("ps", bo)], ["stt"])
                S.op("dve", lambda e, h=h: e.bn_aggr(mvt[:, h, :], stt[:, h, :]), ["stt"], ["mvt"])
            vq = smallf[:, 8:12]
            TT("dve", vq, mvt[:, :, 1], col(C_QD2, 4), ALU.mult, ["mvt"], ["s_vq"])
            TS("dve", vq, vq, EPS, None, ALU.add, None, ["s_vq"], ["s_vq"])
            rs4 = smallf[:, 12:16]
            rsqrt_cols(vq, rs4, 4, ["s_vq"], ["s_rs4"])
            s2 = smallf[:, 16:20]
            TT("dve", s2, rs4, col(C_QD, 4), ALU.mult, ["s_rs4"], ["s_s2"])
            for h in range(NH):
                bo = obanks[h // 2]
                tp, tt = newtmp()
                mg = merged[:, c, h * HD:(h + 1) * HD]
                ACT(tp[:, 0:HD], mg, AF.Copy, [("mg", c, h), "s_s2"], [tt], scale=s2[:, h:h + 1])
                STT(mg, PS[bo][:, (h % 2) * HD:(h % 2 + 1) * HD], mvt[:, h, 0:1], tp[:, 0:HD], ALU.subtract, ALU.mult,
                    [("ps", bo), "mvt", tt], [("mg", c, h)])

        if ti == 0:
            dbg_dump('m1', merged, [("mg", c, q) for c in range(4) for q in range(4)])
        STAGE[0] = 'B_conv'
        if ti == 0:
            S.op("dve", lambda e: e.memset(uT[:, :, 0:30], 0.0), w=[("uTh",)])
        else:
            S.op("act", lambda e: e.activation(uT[:, :, 0:30], uT[:, :, NT:NT + 30], AF.Copy),
                 [("uT", cc) for cc in range(8)], [("uTh",)])
        cur_pool[0] = (0, 1, 2, 3, 4, 5)
        BSUM, BSQ = 6, 7
        slots = {}

        def glu_stage(cc):
            j, m = cc // 4, cc % 4
            if m == 0:
                slots[j] = (W.get(*wblock(w_in, 0, 4096 + j * 512)), W.get(*wblock(w_in, 0, 5120 + j * 512)))
            sa, sg = slots[j]
            ba, bg = bank(), bank()
            for (bb, sl) in ((ba, sa), (bg, sg)):
                for kc in range(8):
                    MM(PS[bb], wbuf[sl][:, kc, m * 128:(m + 1) * 128], actT[:, kc, :], kc == 0, kc == 7,
                       ACT_ALL + [("w", sl)], [("ps", bb)])
            tp, tt = newtmp()
            ACT(tp, PS[bg], AF.Sigmoid, [("ps", bg)], [tt])
            TT("dve", uT[:, cc, 30:30 + NT], PS[ba], tp, ALU.mult, [("ps", ba), tt], [("uT", cc)])

        def conv_stage(cc):
            ds_ = DG.get(dgw[cc], ('dg', cc))
            by = bank()
            for jt in range(CW):
                MM(PS[by], dgbuf[ds_][:, jt, :], uT[:, cc, jt:jt + NT], jt == 0, jt == CW - 1,
                   [("uT", cc), ("uTh",), ("dg", ds_)], [("ps", by)])
            ytok = [("xt", cc // 2)]
            ACT(yv[:, cc, :], PS[by], AF.Identity, [("ps", by)], ytok, bias=col(C_CB + cc))

        def conv_stats(cc):
            ytok = [("xt", cc // 2)]
            tq, tqt = newtmp()
            ACT(tq, yv[:, cc, :], AF.Square, ytok, [tqt])
            MM(PS[BSUM], ones_f, yv[:, cc, :], cc == 0, cc == 7, ytok + ["c_ones"], [("ps", BSUM)])
            MM(PS[BSQ], ones_f, tq, cc == 0, cc == 7, [tqt], [("ps", BSQ)])

        RPOOL, CPOOL = (0, 1, 2), (3, 6, 7)
        RTMP, CTMP = (0, 1, 2), (3, 4, 5)

        def RR(f, *a):
            cur_pool[0] = RPOOL
            tmp_pool[0] = RTMP
            f(*a)

        def CC(f, *a):
            cur_pool[0] = CPOOL
            tmp_pool[0] = CTMP
            f(*a)

        RR(ret_stage1, 0)
        RR(ret_stage2, 0)
        CC(glu_stage, 0)
        for c in range(1, 4):
            RR(ret_gn, c - 1)
            for cc in (2 * (c - 1), 2 * (c - 1) + 1):
                CC(glu_stage, cc + 1)
                CC(conv_stage, cc)
            RR(ret_stage1, c)
            RR(ret_stage2, c)
        RR(ret_gn, 3)
        for cc in (6, 7):
            if cc + 1 < 8:
                CC(glu_stage, cc + 1)
            CC(conv_stage, cc)
        tmp_pool[0] = tuple(range(NTMP))
        if ti == NTILES - 1:
            S.dma("pool", lambda e: e.dma_start(out=srp.rearrange("h (e p) v -> p (h e) v", p=128), in_=S32),
                  r=[("S32", h) for h in range(NH)], w=["o_srp"])

        cur_pool[0] = (0, 1, 2, 3, 4, 5)
        for cc in range(8):
            conv_stats(cc)
        if ti == NTILES - 1:
            for j in range(2):
                sa = W.get(*wblock(w_in, 0, 4096 + j * 512))
                sg = W.get(*wblock(w_in, 0, 5120 + j * 512))
                ba, bg = bank(), bank()
                for (bb, sl) in ((ba, sa), (bg, sg)):
                    for kc in range(8):
                        MM(PS[bb], actT[:, kc, 3 * 128:4 * 128], wbuf[sl][:, kc, :], kc == 0, kc == 7,
                           [("actT", 3), ("w", sl)], [("ps", bb)])
                tp, tt = newtmp()
                ACT(tp, PS[bg], AF.Sigmoid, [("ps", bg)], [tt])
                tu, tut = newtmp()
                TT("dve", tu, PS[ba], tp, ALU.mult, [("ps", ba), tt], [tut])
                S.dma("pool", lambda e, tu=tu, j=j: e.dma_start(out=scp[:, j * 512:(j + 1) * 512], in_=tu[98:128, :]),
                      r=[tut], w=[("o_scp", j)])
        STAGE[0] = 'B_ln'
        mean, mk_ = newtmp()
        msq, qk_ = newtmp()
        lnv, lk_ = newtmp()
        ACT(mean, PS[BSUM], AF.Copy, [("ps", BSUM)], [mk_], scale=1.0 / D)
        ACT(msq, PS[BSUM], AF.Square, [("ps", BSUM)], [qk_], scale=1.0 / D)
        STT(msq, PS[BSQ], 1.0 / D, msq, ALU.mult, ALU.subtract, [("ps", BSQ), qk_], [qk_])
        TS("dve", msq, msq, EPS, None, ALU.add, None, [qk_], [qk_])
        ACT(lnv, msq, AF.Ln, [qk_], [lk_])
        ACT(PS[BSUM], lnv, AF.Exp, [lk_], [("ps", BSUM)], scale=-0.5)
        STT(PS[BSQ], mean, -1.0, PS[BSUM], ALU.mult, ALU.mult, [mk_, ("ps", BSUM)], [("ps", BSQ)])
        zT = v_bf.rearrange("p a b -> p (a b)").rearrange("p (a b) -> p a b", a=8)

        def ln_apply(cc):
            ytok = [("xt", cc // 2)]
            TT("dve", yv[:, cc, :], yv[:, cc, :], PS[BSUM], ALU.mult, ytok + [("ps", BSUM)], ytok)
            TT("dve", yv[:, cc, :], yv[:, cc, :], PS[BSQ], ALU.add, ytok + [("ps", BSQ)], ytok)

        def ln_silu(cc):
            ytok = [("xt", cc // 2)]
            ACT(zT[:, cc, :], yv[:, cc, :], AF.Silu, ytok, [("v", cc // 2)],
                scale=col(C_LNG + cc), bias=col(C_LNB + cc))

        STAGE[0] = 'C_xa'
        for j in range(2):
            def qxcons(m, b, j=j):
                ACT(qxT[:, 4 * j + m, :], PS[b], AF.Copy, [("ps", b)], [("big", 16 + 4 * j + m)])
            fm_block(w_in, 0, 6144 + j * 512, qxcons)
            for hh in range(2):
                h = 2 * j + hh
                for nch in range(2):
                    bx = bank()
                    for e2 in range(2):
                        MM(PS[bx], kTm[:, 2 * h + e2, nch * 128:(nch + 1) * 128], qxT[:, 2 * h + e2, :],
                           e2 == 0, e2 == 1, ["kTm", ("big", 16 + 2 * h), ("big", 16 + 2 * h + 1)], [("ps", bx)])
                    ACT(eT[hh][:, nch, :], PS[bx], AF.Exp, [("ps", bx)], [("eT", hh)], scale=1.0 / 16.0)
            slot = W.get(*wblock(w_in, 0, 9216 + j * 512))
            for c in range(4):
                bg = bank()
                for kc in range(8):
                    MM(PS[bg], actT[:, kc, c * 128:(c + 1) * 128], wbuf[slot][:, kc, :], kc == 0, kc == 7,
                       [("actT", c), ("w", slot)], [("ps", bg)])
                tp, tt = newtmp()
                ACT(tp, PS[bg], AF.Sigmoid, [("ps", bg)], [tt])
                for hh in range(2):
                    h = 2 * j + hh
                    bo = bank()
                    for nch in range(2):
                        MM(PS[bo][:, 0:HD + 1], eT[hh][:, nch, c * 128:(c + 1) * 128], vaug[:, nch, h, :],
                           nch == 0, nch == 1, [("eT", hh), "vaug"], [("ps", bo)])
                    rden = smallf[:, 20 + hh:21 + hh]
                    S.op("dve", lambda e, rden=rden, bo=bo: e.reciprocal(rden, PS[bo][:, HD:HD + 1]),
                         [("ps", bo)], [("s_rden", hh)])
                    tq, tqt = newtmp()
                    STT(tq[:, 0:HD], PS[bo][:, 0:HD], rden, tp[:, hh * HD:(hh + 1) * HD], ALU.mult, ALU.mult,
                        [("ps", bo), ("s_rden", hh), tt], [tqt])
                    mg = merged[:, c, h * HD:(h + 1) * HD]
                    TT("pool", mg, mg, tq[:, 0:HD], ALU.add, [tqt, ("mg", c, h)], [("mg", c, h)])
                ln_apply(4 * j + c)
            for cc in range(4 * j, 4 * j + 4):
                ln_silu(cc)

        if ti == 0:
            dbg_dump('m2', merged, [("mg", c, q) for c in range(4) for q in range(4)])
        STAGE[0] = 'B_out'
        for c in range(4):
            S.dma("sp", lambda e, c=c: e.dma_start(out=xt[:, c, :], in_=xp[t0 + c * 128:t0 + (c + 1) * 128, :]),
                  w=[("xt", c)])
        mb = [tmp[c].bitcast(BF16) for c in range(4)]
        tmp_pool[0] = (4, 5)

        def mergeB(c):
            b = bank()
            for dc in range(8):
                TR(PSB[b][:, dc * 128:(dc + 1) * 128], mb[c][:, dc * 128:(dc + 1) * 128], ident_b, [("tmp", c)], [("ps", b)])
            S.op("dve", lambda e, b=b, c=c: e.tensor_copy(actT[:, :, c * 128:(c + 1) * 128],
                                                         PSB[b].rearrange("p (a b) -> p a b", a=8)),
                 [("ps", b)], [("actT", c)])

        for j in range(2):
            sq_ = W.get(*wblock(w_in, 0, 8192 + j * 512))
            sc_ = W.get(*wblock(w_co, 0, j * 512))
            for c in range(4):
                b1, b2 = bank(), bank()
                for kc in range(8):
                    MM(PS[b1], actT[:, kc, c * 128:(c + 1) * 128], wbuf[sq_][:, kc, :], kc == 0, kc == 7,
                       [("actT", c), ("w", sq_)], [("ps", b1)])
                for kc in range(8):
                    MM(PS[b2], zT[:, kc, c * 128:(c + 1) * 128], wbuf[sc_][:, kc, :], kc == 0, kc == 7,
                       [("v", kc // 2), ("w", sc_)], [("ps", b2)])
                if j == 1 and c >= 1:
                    mergeB(c - 1)
                tp, tt = newtmp()
                ACT(tp, PS[b1], AF.Sigmoid, [("ps", b1)], [tt])
                tq, tqt = newtmp()
                TT("dve", tq, PS[b2], tp, ALU.mult, [("ps", b2), tt], [tqt])
                mg = merged[:, c, j * 512:(j + 1) * 512]
                mt = [("mg", c, 2 * j), ("mg", c, 2 * j + 1)]
                TT("pool", mb[c][:, j * 512:(j + 1) * 512], mg, tq, ALU.add, [tqt] + mt, [("tmp", c)])

        STAGE[0] = 'merge_out'
        mergeB(3)
        tmp_pool[0] = tuple(range(NTMP))
        cur_pool[0] = (0, 1, 2, 3, 4, 5, 6, 7)
        so = [W.get(*wblock(w_out, 0, j * 512)) for j in range(2)]

        def outproj(c):
            for j in range(2):
                b = bank()
                for kc in range(8):
                    MM(PS[b], actT[:, kc, c * 128:(c + 1) * 128], wbuf[so[j]][:, kc, :], kc == 0, kc == 7,
                       [("actT", c), ("w", so[j])], [("ps", b)])
                xv = xt[:, c, j * 512:(j + 1) * 512]
                TT("dve", xv, PS[b], xv, ALU.add, [("ps", b), ("xt", c)], [("xt", c)])

        if ti == 0:
            pass
        STAGE[0] = 'merge_out'
        outproj(0)
        rms_A(xt[:, 0, :], ("xt", 0), 0, 128)
        for c in range(1, 4):
            outproj(c)
            rms_B(c - 1, 128, C_GFFN, actT, ("actT", c - 1))
            rms_A(xt[:, c, :], ("xt", c), c, 128)
        rms_B(3, 128, C_GFFN, actT, ("actT", 3))
        if ti == 0:
            dbg_dump('x2', xt, [('xt', c) for c in range(4)])
        STAGE[0] = 'ffn_up'
        for j in range(8):
            def ucons(m, b, j=j):
                f = 4 * j + m
                tp, tt = newtmp()
                ACT(tp, PS[b], AF.Square, [("ps", b)], [tt])
                STT(rT[:, f, :], PS[b], 0.0, tp, ALU.is_gt, ALU.mult, [("ps", b), tt], [("big", f)])
            fm_block(w_up, 0, j * 512, ucons)
        if ti + 1 < ntiles:
            STAGE[0] = 'prep'
            prep_A(ti + 1, 0)
            prep_A(ti + 1, 1)
        STAGE[0] = 'ffn_down'
        for ch in range(2):
            bks = [4 * ch + c for c in range(4)]
            for fg in range(4):
                slot = W.get(*wblock(w_dn, fg * 1024, ch * 512))
                for c in range(4):
                    for fc in range(8):
                        MM(PS[bks[c]], rT[:, fg * 8 + fc, c * 128:(c + 1) * 128], wbuf[slot][:, fc, :],
                           fg == 0 and fc == 0, fg == 3 and fc == 7, [("big", fg * 8 + fc), ("w", slot)],
                           [("ps", bks[c])])
            for c in range(4):
                xv = xt[:, c, ch * 512:(ch + 1) * 512]
                TT("dve", xv, PS[bks[c]], xv, ALU.add, [("ps", bks[c]), ("xt", c)], [("xt", c)])
            if ti + 1 < ntiles:
                STAGE[0] = 'prep'
                if ch == 0:
                    prep_B(0)
                    prep_B(1)
                    prep_A(ti + 1, 2)
                    prep_A(ti + 1, 3)
                else:
                    prep_B(2)
                    prep_B(3)
                STAGE[0] = 'ffn_down'
        STAGE[0] = 'final'
        for c in range(4):
            x = xt[:, c, :]
            yo = xsb[c % 2]
            ssq = smallf[:, 0:1]
            rs = smallf[:, 1:2]
            lt = smallf[:, 3:4]
            rstd = smallf[:, 2:3]
            ACT(yo, x, AF.Square, [("xt", c)], [("xs", c % 2), "s_ssq"], accum=ssq)
            TS("dve", rs, ssq, 1.0 / D, EPS, ALU.mult, ALU.add, ["s_ssq"], ["s_rs"])
            ACT(lt, rs, AF.Ln, ["s_rs"], ["s_lt"])
            ACT(rstd, lt, AF.Exp, ["s_lt"], ["s_rstd"], scale=-0.5)
            STT(yo, x, rstd, rowbt[:, D:2 * D], ALU.mult, ALU.mult, [("xt", c), "s_rstd"], [("xs", c % 2)])
            S.dma("pool", lambda e, yo=yo, c=c: e.dma_start(out=yp[t0 + c * 128:t0 + (c + 1) * 128, :], in_=yo),
                  r=[("xs", c % 2)], w=[("o_yp", ti, c)])

    def sample_pass():
        STAGE[0] = 'sample'
        S.barrier()
        A.off = mark_common
        NP = NS
        xts = A.tile([D], F32)
        mgs = A.tile([D], F32)
        bigs = A.tile([32, NS], BF16)
        v_s = A.tile([D], BF16)
        khat_s = A.tile([D], BF16)
        S32s = [A.tile([8, HD], F32) for _ in range(3)]
        Sbfs = [A.tile([8, HD], BF16) for _ in range(2)]
        Kb = [A.tile([2, D], BF16) for _ in range(2)]
        vaugs = [A.tile([2, NH, HD + 1], BF16) for _ in range(2)]
        kTs_ = [A.tile([8, NMEM], BF16) for _ in range(2)]
        ext = A.tile([8, SB, 34], BF16)
        stg = [A.tile([D], F32) for _ in range(2)]
        khs = [A.tile([D], BF16) for _ in range(2)]
        qpad = [A.tile([8, NS], BF16) for _ in range(2)]
        eTp = [A.tile([2, NH, NS], BF16) for _ in range(2)]
        sTs = A.tile([NS], BF16)
        yvs = A.tile([8, NS], F32)
        qTs, kTs = bigs[:, 0:8, :], bigs[:, 8:16, :]
        qxTs, zTs, rTs = bigs[:, 16:24, :], bigs[:, 24:32, :], bigs
        hTs = actT[:, :, 0:NS]
        HT = [("actT", 0)]
        cos_, sin_ = cst_s[:, 0, :], cst_s[:, 1, :]
        GEN = (0, 1, 2, 3)

        cur_pool[0] = (0, 1, 2, 3)
        S.dma("sp", lambda e: e.dma_start(out=xts[0:NP], in_=xsm), w=[("xt", 0)])
        rmsnorm_T(lambda c: xts[0:NP], NP, 1, C_GMIX, actT, lambda c: ("xt", 0), lambda c: ("actT", 0))

        def fm_block(wap, r0, c0, consumer):
            slot = W.get(*wblock(wap, r0, c0))
            for m in range(4):
                b = bank()
                for kc in range(8):
                    MM(PS[b][:, 0:NS], wbuf[slot][:, kc, m * 128:(m + 1) * 128], hTs[:, kc, :], kc == 0, kc == 7,
                       HT + [("w", slot)], [("ps", b)])
                consumer(m, b)

        def tm_block(wap, r0, c0, consumer):
            slot = W.get(*wblock(wap, r0, c0))
            b = bank()
            for kc in range(8):
                MM(PS[b][0:NP], hTs[:, kc, :], wbuf[slot][:, kc, :], kc == 0, kc == 7, HT + [("w", slot)], [("ps", b)])
            consumer(b)

        def rope_consumer(dst, base_slab, j):
            st = {}

            def cons(m, b):
                st[m] = b
                if m % 2 == 1:
                    h = 2 * j + m // 2
                    b1, b2 = st[m - 1], st[m]
                    t1, k1 = newtmp()
                    t2, k2 = newtmp()
                    p1, p2 = PS[b1][:, 0:NS], PS[b2][:, 0:NS]
                    TT("dve", t1[:, 0:NS], p1, cos_, ALU.mult, [("ps", b1)], [k1])
                    TT("dve", t2[:, 0:NS], p2, sin_, ALU.mult, [("ps", b2)], [k2])
                    TT("dve", dst[:, 2 * h, :], t1[:, 0:NS], t2[:, 0:NS], ALU.subtract, [k1, k2], [("big", base_slab + 2 * h)])
                    t3, k3 = newtmp()
                    t4, k4 = newtmp()
                    TT("dve", t3[:, 0:NS], p2, cos_, ALU.mult, [("ps", b2)], [k3])
                    TT("dve", t4[:, 0:NS], p1, sin_, ALU.mult, [("ps", b1)], [k4])
                    TT("dve", dst[:, 2 * h + 1, :], t3[:, 0:NS], t4[:, 0:NS], ALU.add, [k3, k4], [("big", base_slab + 2 * h + 1)])
            return cons

        for j in range(2):
            fm_block(w_in, 0, j * 512, rope_consumer(qTs, 0, j))
        for j in range(2):
            fm_block(w_in, 0, 1024 + j * 512, rope_consumer(kTs, 8, j))
        for j in range(2):
            def vcons(b, j=j):
                ACT(v_s[0:NP, j * 512:(j + 1) * 512], PS[b][0:NP], AF.Copy, [("ps", b)], [("v", 0)])
            tm_block(w_in, 0, 2048 + j * 512, vcons)
        MG = lambda q: ("mg", 0, q)
        for j in range(2):
            def gcons(b, j=j):
                tp, tt = newtmp()
                ACT(tp[0:NP], PS[b][0:NP], AF.Silu, [("ps", b)], [tt])
                TT("pool", mgs[0:NP, j * 512:(j + 1) * 512], tp[0:NP], rowbt[0:NP, j * 512:(j + 1) * 512], ALU.mult,
                   [tt], [MG(2 * j), MG(2 * j + 1)])
            tm_block(w_in, 0, 3072 + j * 512, gcons)
        for j in range(2):
            def g0cons(b, j=j):
                tp, tt = newtmp()
                ACT(tp[0:NP], PS[b][0:NP], AF.Sigmoid, [("ps", b)], [tt])
                mg = mgs[0:NP, j * 512:(j + 1) * 512]
                TT("pool", mg, mg, tp[0:NP], ALU.mult, [tt, MG(2 * j), MG(2 * j + 1)], [MG(2 * j), MG(2 * j + 1)])
            tm_block(w_in, 0, 7168 + j * 512, g0cons)

        STAGE[0] = 's_ret'
        OB = (4, 5, 6, 7)
        for h in range(NH):
            bk = bank()
            for e2 in range(2):
                TR(PSB[bk][0:NP, e2 * 128:(e2 + 1) * 128], kTs[:, 2 * h + e2, :], ident_b, [("big", 8 + 2 * h + e2)], [("ps", bk)])
            ACT(khat_s[0:NP, h * HD:(h + 1) * HD], PSB[bk][0:NP, 0:HD], AF.Copy, [("ps", bk)], [("khat", h)],
                scale=cvt[0:NP, C_SKDEC + h:C_SKDEC + h + 1])
            bs = bank()
            for e2 in range(2):
                MM(PS[bs][0:NP, 0:NS], kTs[:, 2 * h + e2, :], qTs[:, 2 * h + e2, :], e2 == 0, e2 == 1,
                   [("big", 2 * h), ("big", 2 * h + 1), ("big", 8 + 2 * h), ("big", 8 + 2 * h + 1)], [("ps", bs)])
            TT("dve", sTs[0:NP], PS[bs][0:NP, 0:NS], maskst[0:NP, h, :], ALU.mult, [("ps", bs)], ["sTs"])
            MM(PS[OB[h]][0:NP, 0:HD], sTs[0:NP], v_s[0:NP, h * HD:(h + 1) * HD], True, False, ["sTs", ("v", 0)], [("ps", OB[h])])
        def load_state(s):
            p3 = s % 3
            S.dma("sp", lambda e: e.dma_start(out=S32s[p3], in_=sret[s].rearrange("h (e p) v -> p (h e) v", p=128)),
                  w=[("S32s", p3)])
        def ret_prep(s):
            par = s % 2
            p3 = s % 3
            ACT(Sbfs[par].rearrange("p a b -> p (a b)"), S32s[p3].rearrange("p a b -> p (a b)"), AF.Copy,
                [("S32s", p3)], [("Sbfs", par)])
            S.op("dve", lambda e, s=s, par=par: e.tensor_tensor(
                qpad[par], qTs, selt[:, s, :].unsqueeze(1).broadcast_to([128, 8, NS]), ALU.mult),
                [("big", q) for q in range(8)], [("qpad", par)])
            ACT(khs[par][0:NP], khat_s[0:NP], AF.Copy, [("khat", h) for h in range(NH)], [("khs", par)],
                scale=cvt[0:NP, C_SEL + s:C_SEL + s + 1])

        def ret_main(s):
            par = s % 2
            p3 = s % 3
            for h in range(NH):
                for e2 in range(2):
                    MM(PS[OB[h]][0:NP, 0:HD], qpad[par][:, 2 * h + e2, :], Sbfs[par][:, 2 * h + e2, :], False,
                       (s == SB - 1 and e2 == 1), [("qpad", par), ("Sbfs", par)], [("ps", OB[h])])
            for h in range(NH):
                bp = bank()
                for e2 in range(2):
                    MM(PS[bp][:, e2 * HD:(e2 + 1) * HD], khs[par][0:NP, h * HD + e2 * 128:h * HD + (e2 + 1) * 128],
                       v_s[0:NP, h * HD:(h + 1) * HD], True, True, [("khs", par), ("v", 0)], [("ps", bp)])
                s32v = S32s[p3][:, 2 * h:2 * h + 2, :].rearrange("p a b -> p (a b)")
                STT(s32v, s32v, G4[h], PS[bp], ALU.mult, ALU.add, [("ps", bp), ("S32s", p3)], [("S32s", p3)])
            S.dma("pool", lambda e, s=s, p3=p3: e.dma_start(out=srs[s].rearrange("h (e p) v -> p (h e) v", p=128), in_=S32s[p3]),
                  r=[("S32s", p3)], w=[("o_srs", s)])

        load_state(0)
        load_state(1)
        ret_prep(0)
        for s in range(SB):
            if s + 2 < SB:
                load_state(s + 2)
            if s + 1 < SB:
                ret_prep(s + 1)
            ret_main(s)
        STAGE[0] = 's_gn'
        for h in range(NH):
            S.op("dve", lambda e, h=h: e.bn_stats(stt[0:NP, h, :], PS[OB[h]][0:NP, 0:HD]), [("ps", OB[h])], ["stt"])
            S.op("dve", lambda e, h=h: e.bn_aggr(mvt[0:NP, h, :], stt[0:NP, h, :]), ["stt"], ["mvt"])
        vq = smallf[0:NP, 8:12]
        TT("dve", vq, mvt[0:NP, :, 1], cvt[0:NP, C_SQD2:C_SQD2 + 4], ALU.mult, ["mvt"], ["s_vq"])
        TS("dve", vq, vq, EPS, None, ALU.add, None, ["s_vq"], ["s_vq"])
        rs4 = smallf[0:NP, 12:16]
        lt4 = smallf[0:NP, 48:52]
        ACT(lt4, vq, AF.Ln, ["s_vq"], ["s_ln"])
        ACT(rs4, lt4, AF.Exp, ["s_ln"], ["s_rs4"], scale=-0.5)
        s2 = smallf[0:NP, 16:20]
        TT("dve", s2, rs4, cvt[0:NP, C_SQD:C_SQD + 4], ALU.mult, ["s_rs4"], ["s_s2"])
        for h in range(NH):
            tp, tt = newtmp()
            mg = mgs[0:NP, h * HD:(h + 1) * HD]
            ACT(tp[0:NP, 0:HD], mg, AF.Copy, [MG(h), "s_s2"], [tt], scale=s2[:, h:h + 1])
            STT(mg, PS[OB[h]][0:NP, 0:HD], mvt[0:NP, h, 0:1], tp[0:NP, 0:HD], ALU.subtract, ALU.mult,
                [("ps", OB[h]), "mvt", tt], [MG(h)])

        STAGE[0] = 's_xa'
        for j in range(2):
            def qxcons(m, b, j=j):
                ACT(qxTs[:, 4 * j + m, :], PS[b][:, 0:NS], AF.Copy, [("ps", b)], [("big", 16 + 4 * j + m)])
            fm_block(w_in, 0, 6144 + j * 512, qxcons)
        for par in range(2):
            S.op("dve", lambda e, par=par: e.memset(vaugs[par][:, :, :, HD:HD + 1], 1.0), w=[("vaugs", par)])
        QX = [("big", 16 + q) for q in range(8)]
        kst = stg
        vst = [A.tile([D], F32) for _ in range(2)]

        def dma_k(s):
            for nch in range(2):
                S.dma("sp", lambda e, nch=nch: e.dma_start(out=kst[nch], in_=ck[s, nch * 128:(nch + 1) * 128, :]),
                      w=[("stg", nch)])

        def cast_k(s):
            par = s % 2
            for nch in range(2):
                ACT(Kb[par][:, nch, :], kst[nch], AF.Copy, [("stg", nch)], [("Kb", par)])

        def dma_v(s):
            for nch in range(2):
                S.dma("sp", lambda e, nch=nch: e.dma_start(out=vst[nch], in_=cv[s, nch * 128:(nch + 1) * 128, :]),
                      w=[("vst", nch)])

        def cast_v(s):
            par = s % 2
            S.op("dve", lambda e, par=par: e.tensor_copy(
                vaugs[par][:, 0, :, 0:HD], vst[0].rearrange("p (h d) -> p h d", h=NH)),
                [("vst", 0)], [("vaugs", par)])
            ACT(vaugs[par][:, 1, :, 0:HD], vst[1].rearrange("p (h d) -> p h d", h=NH), AF.Copy,
                [("vst", 1)], [("vaugs", par)])

        def xa_A(s):
            par = s % 2
            for nch in range(2):
                bt = bank()
                for dc in range(8):
                    TR(PSB[bt][:, dc * 128:(dc + 1) * 128], Kb[par][:, nch, dc * 128:(dc + 1) * 128], ident_b,
                       [("Kb", par)], [("ps", bt)])
                S.op("dve", lambda e, bt=bt, par=par, nch=nch: e.tensor_copy(
                    kTs_[par][:, :, nch * 128:(nch + 1) * 128], PSB[bt].rearrange("p (a b) -> p a b", a=8)),
                    [("ps", bt)], [("kTs_", par)])

        def xa_B1(s):
            par = s % 2
            bx = bank()
            for nch in range(2):
                for h in range(NH):
                    for e2 in range(2):
                        MM(PS[bx][:, (nch * NH + h) * DS:(nch * NH + h + 1) * DS],
                           kTs_[par][:, 2 * h + e2, nch * 128:(nch + 1) * 128], qxTs[:, 2 * h + e2, s * DS:(s + 1) * DS],
                           e2 == 0, e2 == 1, [("kTs_", par)] + QX, [("ps", bx)])
            S.op("dve", lambda e, par=par: e.memset(eTp[par], 0.0), w=[("eTp", par)])
            S.op("act", lambda e, s=s, par=par, bx=bx: e.activation(
                eTp[par][:, :, :, s * DS:(s + 1) * DS], PS[bx][:, 0:2 * NH * DS].rearrange("p (a b c) -> p a b c", a=2, b=NH),
                AF.Exp, scale=1.0 / 16.0), [("ps", bx), ("eTp", par)], [("eTp", par)])

        def xa_B2(s):
            par = s % 2
            for h in range(NH):
                for nch in range(2):
                    MM(PS[OB[h]][0:NP, 0:HD + 1], eTp[par][:, nch, h, :], vaugs[par][:, nch, h, :],
                       (s == 0 and nch == 0), (s == SB - 1 and nch == 1), [("eTp", par), ("vaugs", par)], [("ps", OB[h])])

        for s0 in range(2):
            dma_k(s0)
            cast_k(s0)
            dma_v(s0)
            cast_v(s0)
        xa_A(0)
        for s in range(SB):
            if s + 2 < SB:
                dma_k(s + 2)
            xa_B1(s)
            if s + 1 < SB:
                xa_A(s + 1)
            xa_B2(s)
            if s + 2 < SB:
                cast_k(s + 2)
                dma_v(s + 2)
                cast_v(s + 2)
        for j in range(2):
            def g2cons(b, j=j):
                tp, tt = newtmp()
                ACT(tp[0:NP], PS[b][0:NP], AF.Sigmoid, [("ps", b)], [tt])
                for hh in range(2):
                    h = 2 * j + hh
                    rden = smallf[0:NP, 20 + hh:21 + hh]
                    S.op("dve", lambda e, rden=rden, h=h: e.reciprocal(rden, PS[OB[h]][0:NP, HD:HD + 1]),
                         [("ps", OB[h])], [("s_rden", hh)])
                    tq, tqt = newtmp()
                    STT(tq[0:NP, 0:HD], PS[OB[h]][0:NP, 0:HD], rden, tp[0:NP, hh * HD:(hh + 1) * HD], ALU.mult, ALU.mult,
                        [("ps", OB[h]), ("s_rden", hh), tt], [tqt])
                    mg = mgs[0:NP, h * HD:(h + 1) * HD]
                    TT("pool", mg, mg, tq[0:NP, 0:HD], ALU.add, [tqt, MG(h)], [MG(h)])
            tm_block(w_in, 0, 9216 + j * 512, g2cons)

        STAGE[0] = 's_conv'
        for s in range(SB):
            S.dma("sp", lambda e, s=s: e.dma_start(out=scs[s, 0:26, :], in_=sconv[s, 4:30, :]), w=[("o_scs_h", s)])
        for g in range(4):
            par = g % 2
            S.dma("sp", lambda e, g=g, par=par: e.dma_start(
                out=stg[par][0:120], in_=sconv[4 * g:4 * g + 4].rearrange("s r f -> (s r) f")), w=[("stg", par)])
            tp, tt = newtmp()
            sb16 = tp.bitcast(BF16)
            ACT(sb16[0:120], stg[par][0:120], AF.Copy, [("stg", par)], [tt])
            bt = bank()
            for cc in range(8):
                TR(PSB[bt][:, cc * 128:cc * 128 + 120], sb16[0:120, cc * 128:(cc + 1) * 128], ident_b[0:120, 0:120],
                   [tt], [("ps", bt)])
            for cc in range(8):
                S.op("dve", lambda e, bt=bt, cc=cc, g=g: e.tensor_copy(
                    ext[:, cc, 4 * g:4 * g + 4, 0:30], PSB[bt][:, cc * 128:cc * 128 + 120].rearrange("p (a b) -> p a b", a=4)),
                    [("ps", bt)], [("ext", cc)])
        BSUM, BSQ = 4, 5
        for j in range(2):
            sa = W.get(*wblock(w_in, 0, 4096 + j * 512))
            sg = W.get(*wblock(w_in, 0, 5120 + j * 512))
            ba, bg = bank(), bank()
            for (bb, sl) in ((ba, sa), (bg, sg)):
                for kc in range(8):
                    MM(PS[bb][0:NP], hTs[:, kc, :], wbuf[sl][:, kc, :], kc == 0, kc == 7, HT + [("w", sl)], [("ps", bb)])
            tp, tt = newtmp()
            ACT(tp[0:NP], PS[bg][0:NP], AF.Sigmoid, [("ps", bg)], [tt])
            tu, tut = newtmp()
            TT("dve", tu[0:NP], PS[ba][0:NP], tp[0:NP], ALU.mult, [("ps", ba), tt], [tut])
            for s in range(SB):
                S.dma("pool", lambda e, tu=tu, j=j, s=s: e.dma_start(
                    out=scs[s, 26:30, j * 512:(j + 1) * 512], in_=tu[s * DS:(s + 1) * DS, :]),
                    r=[tut], w=[("o_scs_u", s, j)])
            for m in range(4):
                cc = 4 * j + m
                ba, bg = bank(), bank()
                for (bb, sl) in ((ba, sa), (bg, sg)):
                    for kc in range(8):
                        MM(PS[bb][:, 0:NS], wbuf[sl][:, kc, m * 128:(m + 1) * 128], hTs[:, kc, :], kc == 0, kc == 7,
                           HT + [("w", sl)], [("ps", bb)])
                tp, tt = newtmp()
                ACT(tp[:, 0:NS], PS[bg][:, 0:NS], AF.Sigmoid, [("ps", bg)], [tt])
                S.op("dve", lambda e, cc=cc, ba=ba, tp=tp: e.tensor_tensor(
                    ext[:, cc, :, 30:34], PS[ba][:, 0:NS].rearrange("p (a b) -> p a b", a=SB),
                    tp[:, 0:NS].rearrange("p (a b) -> p a b", a=SB), ALU.mult), [("ps", ba), tt], [("ext", cc)])
                ds_ = DG.get(dgw[cc], ('dg', cc))
                by = bank()
                for jt in range(CW):
                    MM(PS[by][:, 0:NS], dgbuf[ds_][:, jt, :], ext[:, cc, :, jt:jt + DS], jt == 0, jt == CW - 1,
                       [("ext", cc), ("dg", ds_)], [("ps", by)])
                ACT(yvs[:, cc, :], PS[by][:, 0:NS], AF.Identity, [("ps", by)], [("yvs", cc)], bias=col(C_CB + cc))
                tq, tqt = newtmp()
                ACT(tq[:, 0:NS], yvs[:, cc, :], AF.Square, [("yvs", cc)], [tqt])
                MM(PS[BSUM][:, 0:NS], ones_f, yvs[:, cc, :], cc == 0, cc == 7, [("yvs", cc)], [("ps", BSUM)])
                MM(PS[BSQ][:, 0:NS], ones_f, tq[:, 0:NS], cc == 0, cc == 7, [tqt], [("ps", BSQ)])
        mean, mk_ = newtmp()
        msq, qk_ = newtmp()
        rstd, rk_ = newtmp()
        nmr, nk_ = newtmp()
        mean, msq, rstd, nmr = mean[:, 0:NS], msq[:, 0:NS], rstd[:, 0:NS], nmr[:, 0:NS]
        ACT(mean, PS[BSUM][:, 0:NS], AF.Copy, [("ps", BSUM)], [mk_], scale=1.0 / D)
        ACT(msq, PS[BSUM][:, 0:NS], AF.Square, [("ps", BSUM)], [qk_], scale=1.0 / D)
        STT(msq, PS[BSQ][:, 0:NS], 1.0 / D, msq, ALU.mult, ALU.subtract, [("ps", BSQ), qk_], [qk_])
        TS("dve", msq, msq, EPS, None, ALU.add, None, [qk_], [qk_])
        ACT(rstd, msq, AF.Ln, [qk_], [rk_])
        ACT(rstd, rstd, AF.Exp, [rk_], [rk_], scale=-0.5)
        STT(nmr, mean, -1.0, rstd, ALU.mult, ALU.mult, [mk_, rk_], [nk_])
        for cc in range(8):
            TT("dve", yvs[:, cc, :], yvs[:, cc, :], rstd, ALU.mult, [("yvs", cc), rk_], [("yvs", cc)])
            TT("pool", yvs[:, cc, :], yvs[:, cc, :], nmr, ALU.add, [("yvs", cc), nk_], [("yvs", cc)])
            ACT(zTs[:, cc, :], yvs[:, cc, :], AF.Silu, [("yvs", cc)], [("big", 24 + cc)],
                scale=col(C_LNG + cc), bias=col(C_LNB + cc))
        for j in range(2):
            sq_ = W.get(*wblock(w_in, 0, 8192 + j * 512))
            sc_ = W.get(*wblock(w_co, 0, j * 512))
            b1, b2 = bank(), bank()
            for kc in range(8):
                MM(PS[b1][0:NP], hTs[:, kc, :], wbuf[sq_][:, kc, :], kc == 0, kc == 7, HT + [("w", sq_)], [("ps", b1)])
            for kc in range(8):
                MM(PS[b2][0:NP], zTs[:, kc, :], wbuf[sc_][:, kc, :], kc == 0, kc == 7,
                   [("big", 24 + kc), ("w", sc_)], [("ps", b2)])
            tp, tt = newtmp()
            ACT(tp[0:NP], PS[b1][0:NP], AF.Sigmoid, [("ps", b1)], [tt])
            tq, tqt = newtmp()
            TT("dve", tq[0:NP], PS[b2][0:NP], tp[0:NP], ALU.mult, [("ps", b2), tt], [tqt])
            mg = mgs[0:NP, j * 512:(j + 1) * 512]
            mt = [MG(2 * j), MG(2 * j + 1)]
            TT("pool", mg, mg, tq[0:NP], ALU.add, [tqt] + mt, mt)

        cur_pool[0] = (0, 1, 2, 3)
        STAGE[0] = 's_tail'
        tp, tt = newtmp()
        mbv = tp.bitcast(BF16)
        ACT(mbv[0:NP], mgs[0:NP], AF.Copy, [MG(q) for q in range(4)], [tt])
        b = bank()
        for dc in range(8):
            TR(PSB[b][:, dc * 128:dc * 128 + NP], mbv[0:NP, dc * 128:(dc + 1) * 128], ident_b[0:NP, 0:NP], [tt], [("ps", b)])
        S.op("dve", lambda e, b=b: e.tensor_copy(actT[:, :, 0:NS], PSB[b].rearrange("p (a b) -> p a b", a=8)[:, :, 0:NP]),
             [("ps", b)], [("actT", 0)])
        for j in range(2):
            def ocons(b, j=j):
                xv = xts[0:NP, j * 512:(j + 1) * 512]
                TT("dve", xv, PS[b][0:NP], xv, ALU.add, [("ps", b), ("xt", 0)], [("xt", 0)])
            tm_block(w_out, 0, j * 512, ocons)
        rmsnorm_T(lambda c: xts[0:NP], NP, 1, C_GFFN, actT, lambda c: ("xt", 0), lambda c: ("actT", 0))
        for j in range(8):
            def ucons(m, b, j=j):
                f = 4 * j + m
                tp, tt = newtmp()
                ACT(tp[:, 0:NS], PS[b][:, 0:NS], AF.Square, [("ps", b)], [tt])
                STT(rTs[:, f, :], PS[b][:, 0:NS], 0.0, tp[:, 0:NS], ALU.is_gt, ALU.mult, [("ps", b), tt], [("big", f)])
            fm_block(w_up, 0, j * 512, ucons)
        for ch in range(2):
            bkk = 4 + ch
            for fg in range(4):
                slot = W.get(*wblock(w_dn, fg * 1024, ch * 512))
                for fc in range(8):
                    MM(PS[bkk][0:NP], rTs[:, fg * 8 + fc, :], wbuf[slot][:, fc, :], fg == 0 and fc == 0,
                       fg == 3 and fc == 7, [("big", fg * 8 + fc), ("w", slot)], [("ps", bkk)])
            xv = xts[0:NP, ch * 512:(ch + 1) * 512]
            TT("dve", xv, PS[bkk][0:NP], xv, ALU.add, [("ps", bkk), ("xt", 0)], [("xt", 0)])
        x = xts[0:NP]
        yo = xsb[0][0:NP]
        ssq, rs, rstd_, lt = smallf[0:NP, 0:1], smallf[0:NP, 1:2], smallf[0:NP, 2:3], smallf[0:NP, 3:4]
        ACT(yo, x, AF.Square, [("xt", 0)], [("xs", 0), "s_ssq"], accum=ssq)
        TS("dve", rs, ssq, 1.0 / D, EPS, ALU.mult, ALU.add, ["s_ssq"], ["s_rs"])
        ACT(lt, rs, AF.Ln, ["s_rs"], ["s_lt"])
        ACT(rstd_, lt, AF.Exp, ["s_lt"], ["s_rstd"], scale=-0.5)
        STT(yo, x, rstd_, rowbt[0:NP, D:2 * D], ALU.mult, ALU.mult, [("xt", 0), "s_rstd"], [("xs", 0)])
        S.dma("pool", lambda e: e.dma_start(out=ysm, in_=yo), r=[("xs", 0)], w=["o_ysm"])

    S.op("dve", lambda e: e.memset(S32, 0.0), w=[("S32", h) for h in range(NH)])
    S.op("dve", lambda e: e.memset(Sbf, 0.0), w=[("Sbf", h) for h in range(NH)])
    memkv()
    for ti in range(ntiles):
        prompt_tile(ti)
    if int(os.environ.get('K_SAMPLE', 1)):
        sample_pass()
    print("arena used words:", A.hi, "of", NW, " ops:", len(S.ops))
    S.analyze()
    LAST_SCHED[0] = S
    S.emit(nc, es)
    es.close()
    return nc


def host_consts(inp):
    f32 = np.float32
    lg = log_gammas()
    c = {}
    cvec = np.zeros((128, NCV), f32)

    def colfill(c0, vec):
        cvec[:, c0:c0 + 8] = np.asarray(vec, f32).reshape(8, 128).T
    colfill(0, inp["g_mix"][0])
    colfill(8, inp["g_ffn"][0])
    colfill(16, inp["g_mem"][0])
    colfill(24, inp["conv_b"][0])
    colfill(32, inp["conv_ln_g"][0])
    colfill(40, inp["conv_ln_b"][0])
    p = np.arange(128, dtype=np.float64)
    for h in range(NH):
        cvec[:, 48 + h] = np.exp(lg[h] * (p + 1))
        cvec[:, 52 + h] = np.exp(2 * lg[h] * (p + 1))
        cvec[:, 56 + h] = np.exp(lg[h] * (127 - p)) / 16.0
        t = (np.arange(128) % DS).astype(np.float64)
        cvec[:, 60 + h] = np.exp(lg[h] * (t + 1))
        cvec[:, 64 + h] = np.exp(2 * lg[h] * (t + 1))
        cvec[:, 68 + h] = np.exp(lg[h] * (DS - 1 - t)) / 16.0
    for s in range(SB):
        cvec[:, 72 + s] = ((np.arange(128) // DS) == s).astype(f32)
    c["cvec"] = cvec
    c["rowb"] = np.ascontiguousarray(np.broadcast_to(
        np.concatenate([inp["ret_gn_g"][0], inp["g_final"]]).astype(f32)[None, :], (128, 2 * D)))
    inv = (f32(10000.0) ** (-(np.arange(0, HD, 2).astype(f32)) / f32(HD))).astype(f32)
    pos = np.arange(T).astype(f32)
    ang = (pos[None, :] * inv[:, None]).astype(f32)
    c["cosp"] = np.cos(ang.astype(np.float64)).astype(f32)
    c["sinp"] = np.sin(ang.astype(np.float64)).astype(f32)
    poss = (PAST + (np.arange(NS) % DS)).astype(f32)
    angs = (poss[None, :] * inv[:, None]).astype(f32)
    c["css"] = np.ascontiguousarray(np.stack([np.cos(angs.astype(np.float64)), np.sin(angs.astype(np.float64))], 1).astype(f32))
    jj = np.arange(128)[:, None]
    ii = np.arange(128)[None, :]
    maskp = np.zeros((128, NH, 128), f32)
    masks = np.zeros((128, NH, 64), f32)
    js = np.arange(64)[:, None]
    is_ = np.arange(64)[None, :]
    for h in range(NH):
        maskp[:, h, :] = np.where(ii >= jj, np.exp(-lg[h] * (jj + 1)) / 16.0, 0.0)
        same = (js // DS) == (is_ // DS)
        masks[:64, h, :] = np.where(same & ((is_ % DS) >= (js % DS)), np.exp(-lg[h] * ((js % DS) + 1)) / 16.0, 0.0)
    c["maskp"] = maskp
    c["masks"] = masks
    c["identf"] = np.eye(128, dtype=f32)
    selm = np.zeros((128, SB, 64), f32)
    for s in range(SB):
        selm[:, s, s * DS:(s + 1) * DS] = 1.0
    c["selm"] = selm
    cw = np.asarray(inp["conv_w"][0], f32)
    dgw = np.zeros((8, 128, CW, 128), f32)
    ar = np.arange(128)
    for cc in range(8):
        dgw[cc, ar, :, ar] = cw[:, cc * 128:(cc + 1) * 128].T
    c["dgw"] = dgw
    return c


_NC = None


def kernel(x_prompt, x_sample, mem_prompt, state_ret, state_conv, cache_mem_k, cache_mem_v,
           g_mix, w_in, ret_gn_g, conv_w, conv_b, conv_ln_g, conv_ln_b, w_conv_out, w_out,
           g_ffn, w_up, w_down, g_mem, w_mem_kv, g_final):
    global _NC
    f32 = np.float32
    inp = dict(g_mix=g_mix, ret_gn_g=ret_gn_g, conv_w=conv_w, conv_b=conv_b, conv_ln_g=conv_ln_g,
               conv_ln_b=conv_ln_b, g_ffn=g_ffn, g_mem=g_mem, g_final=g_final)
    inp = {k: np.asarray(v, f32) for k, v in inp.items()}
    hc = host_consts(inp)
    if _NC is None:
        _NC = build()
    nc = _NC
    shared = dict(
        w_in=np.ascontiguousarray(np.asarray(w_in, f32)[0]),
        w_co=np.ascontiguousarray(np.asarray(w_conv_out, f32)[0]),
        w_out=np.ascontiguousarray(np.asarray(w_out, f32)[0]),
        w_up=np.ascontiguousarray(np.asarray(w_up, f32)[0]),
        w_dn=np.ascontiguousarray(np.asarray(w_down, f32)[0]),
        w_kv=np.ascontiguousarray(np.asarray(w_mem_kv, f32)[0]),
        **hc)
    x_prompt = np.asarray(x_prompt, f32)
    x_sample = np.asarray(x_sample, f32)
    mem_prompt = np.asarray(mem_prompt, f32)
    state_ret = np.asarray(state_ret, f32)
    state_conv = np.asarray(state_conv, f32)
    cache_mem_k = np.asarray(cache_mem_k, f32)
    cache_mem_v = np.asarray(cache_mem_v, f32)
    in_maps = []
    for i in range(NCORES):
        sl = slice(i * SB, (i + 1) * SB)
        m = dict(shared)
        m["xp"] = np.ascontiguousarray(x_prompt[i])
        m["xsm"] = np.ascontiguousarray(x_sample[sl].reshape(NS, D))
        m["mem"] = np.ascontiguousarray(mem_prompt[i])
        m["sret"] = np.ascontiguousarray(state_ret[0, sl])
        m["sconv"] = np.ascontiguousarray(state_conv[0, sl])
        m["ck"] = np.ascontiguousarray(cache_mem_k[0, sl].reshape(SB, NMEM, D))
        m["cv"] = np.ascontiguousarray(cache_mem_v[0, sl].reshape(SB, NMEM, D))
        in_maps.append(m)
    res = run_bass_kernel_spmd(nc, in_maps, core_ids=list(range(NCORES)))
    R = res.results
    for k in R[0]:
        if k.startswith('dbg_'):
            DEBUG_OUT[k] = np.asarray(R[0][k])
    y_prompt = np.stack([R[i]["yp"] for i in range(NCORES)], 0)
    y_sample = np.concatenate([R[i]["ysm"].reshape(SB, DS, D) for i in range(NCORES)], 0)
    srp_o = np.stack([R[i]["srp"] for i in range(NCORES)], 0)[None]
    scp_o = np.stack([R[i]["scp"] for i in range(NCORES)], 0)[None]
    mk_o = np.stack([R[i]["mkp"].reshape(NMEM, NH, HD) for i in range(NCORES)], 0)[None]
    mv_o = np.stack([R[i]["mvp"].reshape(NMEM, NH, HD) for i in range(NCORES)], 0)[None]
    srs_o = np.concatenate([R[i]["srs"] for i in range(NCORES)], 0)[None]
    scs_o = np.concatenate([R[i]["scs"] for i in range(NCORES)], 0)[None]
    return (y_prompt.astype(f32), y_sample.astype(f32), srp_o.astype(f32), scp_o.astype(f32),
            mk_o.astype(f32), mv_o.astype(f32), srs_o.astype(f32), scs_o.astype(f32))
```
